# Optimizing a Trainium2 kernel written in Bass

```python
import math
import jax, jax.numpy as jnp
from jax import lax
import numpy as np

D_MODEL = 2048
BATCH = 2
SEQ = 16384
DEPTH = 1

DA_HEADS = 8
DA_HEAD_DIM = 64
DA_V_DIM = 2 * DA_HEAD_DIM
DA_QK_WIDTH = DA_HEADS * 2 * DA_HEAD_DIM
DA_WIDTH = DA_HEADS * DA_V_DIM
ROPE_THETA = 500000.0
ROPE_DIM = DA_HEAD_DIM // 4
Q_BLOCK = 128
GLA_HEADS = 4
GLA_KEY_DIM = 128
GLA_VAL_DIM = 256
GLA_QK_WIDTH = GLA_HEADS * GLA_KEY_DIM
GLA_WIDTH = GLA_HEADS * GLA_VAL_DIM
GLA_GATE_RANK = 16
GLA_GATE_TAU = 16.0
GLA_CHUNK = 64
D_FF = 4 * D_MODEL
N_MOD = 6
EPS = 1e-6

IN_SIZES = (DA_QK_WIDTH, DA_QK_WIDTH, DA_WIDTH,
            GLA_QK_WIDTH, GLA_QK_WIDTH, GLA_WIDTH, GLA_WIDTH, GLA_GATE_RANK,
            D_MODEL, D_MODEL)
IN_WIDTH = (3 * DA_QK_WIDTH + 2 * GLA_QK_WIDTH + 2 * GLA_WIDTH + GLA_GATE_RANK + 2 * D_MODEL)

kernel_name = "hybrid_diffattn_gla_gated_block"


def _split_points():
    pts, acc = [], 0
    for s in IN_SIZES[:-1]:
        acc += s
        pts.append(acc)
    return pts


def rms_norm(x, g):
    xf = x.astype(jnp.float32)
    y = xf * lax.rsqrt(jnp.mean(xf * xf, axis=-1, keepdims=True) + EPS)
    return (y * g.astype(jnp.float32)).astype(x.dtype)


def partial_rope(t, cos, sin):
    half = ROPE_DIM // 2
    tr, tp = t[..., :ROPE_DIM], t[..., ROPE_DIM:]
    rot = jnp.concatenate([-tr[..., half:], tr[..., :half]], axis=-1)
    return jnp.concatenate([tr * cos + rot * sin, tp], axis=-1)


def diff_attention(q, k, v, positions, q_norm_g, k_norm_g, lq1, lk1, lq2, lk2, subln_g, layer_idx):
    B, S = q.shape[0], q.shape[1]
    f32 = jnp.float32
    q = rms_norm(q.reshape(B, S, DA_HEADS, 2, DA_HEAD_DIM).astype(f32), q_norm_g)
    k = rms_norm(k.reshape(B, S, DA_HEADS, 2, DA_HEAD_DIM).astype(f32), k_norm_g)
    vf = v.reshape(B, S, DA_HEADS, DA_V_DIM).astype(f32)
    inv_freq = ROPE_THETA ** (-jnp.arange(0, ROPE_DIM, 2, dtype=f32) / ROPE_DIM)
    ang = positions.astype(f32)[..., None] * inv_freq
    ang = jnp.concatenate([ang, ang], axis=-1)[:, :, None, None, :]
    cos, sin = jnp.cos(ang), jnp.sin(ang)
    q = partial_rope(q, cos, sin) * (DA_HEAD_DIM ** -0.5)
    k = partial_rope(k, cos, sin)
    lam_init = 0.8 - 0.6 * math.exp(-0.3 * layer_idx)
    lam = (jnp.exp(jnp.sum(lq1.astype(f32) * lk1.astype(f32)))
           - jnp.exp(jnp.sum(lq2.astype(f32) * lk2.astype(f32))) + lam_init)
    nb = S // Q_BLOCK
    q_blocks = q.reshape(B, nb, Q_BLOCK, DA_HEADS, 2, DA_HEAD_DIM).transpose(1, 0, 2, 3, 4, 5)
    key_idx = jnp.arange(S)

    def one_block(args):
        qb, bi = args
        s = jnp.einsum('bqhcd,bkhcd->bhcqk', qb, k)
        q_idx = bi * Q_BLOCK + jnp.arange(Q_BLOCK)
        mask = key_idx[None, :] <= q_idx[:, None]
        p = jax.nn.softmax(jnp.where(mask, s, -jnp.inf), axis=-1)
        w = p[:, :, 0] - lam * p[:, :, 1]
        return jnp.einsum('bhqk,bkhe->bqhe', w, vf)

    o = lax.map(one_block, (q_blocks, jnp.arange(nb)))
    o = o.transpose(1, 0, 2, 3, 4).reshape(B, S, DA_HEADS, DA_V_DIM)
    o = rms_norm(o, subln_g) * (1.0 - lam_init)
    return o.reshape(B, S, DA_WIDTH)


def gated_linear_attention(q, k, v, r, a_low, a_up, a_bias, out_norm_g):
    B, S = q.shape[0], q.shape[1]
    f32 = jnp.float32
    C = GLA_CHUNK
    nc = S // C
    q = q.reshape(B, S, GLA_HEADS, GLA_KEY_DIM).astype(f32) * (GLA_KEY_DIM ** -0.5)
    k = k.reshape(B, S, GLA_HEADS, GLA_KEY_DIM).astype(f32)
    v = v.reshape(B, S, GLA_HEADS, GLA_VAL_DIM).astype(f32)
    log_a = jax.nn.log_sigmoid((a_low @ a_up + a_bias).astype(f32)) / GLA_GATE_TAU
    log_a = log_a.reshape(B, S, GLA_HEADS, GLA_KEY_DIM)

    def to_chunks(t):
        return t.reshape(B, nc, C, *t.shape[2:]).swapaxes(0, 1)

    causal = jnp.tril(jnp.ones((C, C), dtype=bool))

    def step(state, inp):
        qc, kc, vc, lac = inp
        b = jnp.cumsum(lac, axis=1)
        o_inter = jnp.einsum('bihk,bhkv->bihv', qc * jnp.exp(b), state)
        rel = b[:, :, None] - b[:, None]
        decay = jnp.exp(jnp.where(causal[None, :, :, None, None], rel, -jnp.inf))
        att = jnp.einsum('bihk,bjhk,bijhk->bhij', qc, kc, decay)
        o_intra = jnp.einsum('bhij,bjhv->bihv', att, vc)
        b_last = b[:, -1]
        state = (jnp.exp(b_last)[..., None] * state
                 + jnp.einsum('bjhk,bjhv->bhkv', kc * jnp.exp(b_last[:, None] - b), vc))
        return state, o_inter + o_intra

    state0 = jnp.zeros((B, GLA_HEADS, GLA_KEY_DIM, GLA_VAL_DIM), f32)
    _, o = lax.scan(step, state0, (to_chunks(q), to_chunks(k), to_chunks(v), to_chunks(log_a)))
    o = o.swapaxes(0, 1).reshape(B, S, GLA_HEADS, GLA_VAL_DIM)
    o = rms_norm(o, out_norm_g) * jax.nn.silu(r.astype(f32)).reshape(B, S, GLA_HEADS, GLA_VAL_DIM)
    return o.reshape(B, S, GLA_WIDTH)


def setup_inputs(seed: int = 0) -> dict:
    key = jax.random.key(seed)
    ks = jax.random.split(key, 24)
    f32 = jnp.float32
    L, D = DEPTH, D_MODEL

    def nrm(k, shape, scale):
        return jax.random.normal(k, shape, f32) * scale

    def gain(k, shape):
        return 1.0 + 0.02 * jax.random.normal(k, shape, f32)

    x = jax.random.normal(ks[0], (BATCH, SEQ, D), f32)
    c = jax.random.normal(ks[1], (BATCH, D), f32)
    positions = (jnp.arange(SEQ, dtype=jnp.int32)[None, :]
                 + jax.random.randint(ks[2], (BATCH, 1), 0, 1024, dtype=jnp.int32))
    return {
        "x": x,
        "c": c,
        "positions": positions,
        "w_ada": nrm(ks[3], (L, D, N_MOD * D), 0.5 * D ** -0.5),
        "b_ada": nrm(ks[4], (L, N_MOD * D), 0.02),
        "norm1_g": gain(ks[5], (L, D)),
        "w_in": nrm(ks[6], (L, D, IN_WIDTH), D ** -0.5),
        "da_q_norm_g": gain(ks[7], (L, DA_HEAD_DIM)),
        "da_k_norm_g": gain(ks[8], (L, DA_HEAD_DIM)),
        "da_lambda_q1": nrm(ks[9], (L, DA_HEAD_DIM), 0.1),
        "da_lambda_k1": nrm(ks[10], (L, DA_HEAD_DIM), 0.1),
        "da_lambda_q2": nrm(ks[11], (L, DA_HEAD_DIM), 0.1),
        "da_lambda_k2": nrm(ks[12], (L, DA_HEAD_DIM), 0.1),
        "da_subln_g": gain(ks[13], (L, DA_V_DIM)),
        "gla_gate_up": nrm(ks[14], (L, GLA_GATE_RANK, GLA_QK_WIDTH), GLA_GATE_RANK ** -0.5),
        "gla_gate_bias": nrm(ks[15], (L, GLA_QK_WIDTH), 0.1),
        "gla_out_norm_g": gain(ks[16], (L, GLA_VAL_DIM)),
        "w_branch_a": nrm(ks[17], (L, DA_WIDTH, D), DA_WIDTH ** -0.5),
        "w_branch_b": nrm(ks[18], (L, GLA_WIDTH, D), GLA_WIDTH ** -0.5),
        "w_out": nrm(ks[19], (L, D, D), D ** -0.5),
        "norm2_g": gain(ks[20], (L, D)),
        "w_mlp_in": nrm(ks[21], (L, D, D_FF), D ** -0.5),
        "w_mlp_out": nrm(ks[22], (L, D_FF, D), D_FF ** -0.5),
    }


def reference(x, c, positions, w_ada, b_ada, norm1_g, w_in, da_q_norm_g, da_k_norm_g,
              da_lambda_q1, da_lambda_k1, da_lambda_q2, da_lambda_k2, da_subln_g,
              gla_gate_up, gla_gate_bias, gla_out_norm_g, w_branch_a, w_branch_b, w_out,
              norm2_g, w_mlp_in, w_mlp_out):
    h = x
    split_pts = _split_points()
    for l in range(DEPTH):
        mod = jax.nn.silu(c) @ w_ada[l] + b_ada[l]
        shift1, scale1, gate1, shift2, scale2, gate2 = jnp.split(mod[:, None, :], N_MOD, axis=-1)
        u = rms_norm(h, norm1_g[l]) * (1.0 + scale1) + shift1
        proj = u @ w_in[l]
        (da_q, da_k, da_v, g_q, g_k, g_v, g_r, g_a, gate_a, gate_b) = jnp.split(proj, split_pts, axis=-1)
        y_a = diff_attention(da_q, da_k, da_v, positions, da_q_norm_g[l], da_k_norm_g[l],
                             da_lambda_q1[l], da_lambda_k1[l], da_lambda_q2[l], da_lambda_k2[l],
                             da_subln_g[l], l).astype(h.dtype)
        y_b = gated_linear_attention(g_q, g_k, g_v, g_r, g_a, gla_gate_up[l], gla_gate_bias[l],
                                     gla_out_norm_g[l]).astype(h.dtype)
        merged = (jax.nn.sigmoid(gate_a) * (y_a @ w_branch_a[l])
                  + jax.nn.sigmoid(gate_b) * (y_b @ w_branch_b[l]))
        h = h + gate1 * (merged @ w_out[l])
        u2 = rms_norm(h, norm2_g[l]) * (1.0 + scale2) + shift2
        hid = jnp.square(jax.nn.relu(u2 @ w_mlp_in[l]))
        h = h + gate2 * (hid @ w_mlp_out[l])
    return h
```

```python
import math
from contextlib import ExitStack
import numpy as np
import ml_dtypes
import concourse.bass as bass
import concourse.mybir as mybir
from concourse.bass_utils import run_bass_kernel_spmd

F32 = mybir.dt.float32
BF16 = mybir.dt.bfloat16
I32 = mybir.dt.int32
AF = mybir.ActivationFunctionType
ALU = mybir.AluOpType
AX = mybir.AxisListType

D = 2048
NCH = 16
EPS = 1e-6
TWO_PI = 2.0 * math.pi
MAGIC = 12582912.0
CW1 = 6.28125
CW2 = TWO_PI - 6.28125


class Buf:
    __slots__ = ("name", "w", "r", "excl")

    def __init__(self, name="b", excl=False):
        self.name = name
        self.w = None
        self.r = []
        self.excl = excl


class Sched:
    def __init__(self, nc):
        self.nc = nc
        self.E = {"pe": nc.tensor, "act": nc.scalar, "dve": nc.vector, "pool": nc.gpsimd, "sp": nc.sync}
        self.sem = {k: nc.alloc_semaphore("s_" + k) for k in ("pe", "act", "dve", "pool")}
        self.cnt = {k: 0 for k in self.sem}
        self.waited = {k: {} for k in self.E}
        self.dma_sems = []
        self.rr = 0

    def _deps(self, eng, reads, writes):
        best = {}
        for b in reads:
            if b.w is not None:
                s, v = b.w
                if best.get(s, 0) < v:
                    best[s] = v
        for b in writes:
            if b.w is not None:
                s, v = b.w
                if best.get(s, 0) < v:
                    best[s] = v
            for (s, v) in b.r:
                if best.get(s, 0) < v:
                    best[s] = v
        wd = self.waited[eng]
        e = self.E[eng]
        for s, v in best.items():
            if wd.get(s, 0) < v:
                e.wait_ge(s, v)
                wd[s] = v

    def op(self, eng, fn, reads=(), writes=(), signal=True):
        if any(b.excl for b in reads):
            writes = list(writes) + [b for b in reads if b.excl]
            reads = [b for b in reads if not b.excl]
        self._deps(eng, reads, writes)
        ins = fn()
        if signal:
            self.cnt[eng] += 1
            ins.then_inc(self.sem[eng], 1)
            tok = (self.sem[eng], self.cnt[eng])
            for b in reads:
                b.r.append(tok)
                if len(b.r) > 12:
                    b.r = self._compact(b.r)
            for b in writes:
                b.w = tok
                b.r = []
        return ins

    @staticmethod
    def _compact(lst):
        best = {}
        for s, v in lst:
            if best.get(s, 0) < v:
                best[s] = v
        return list(best.items())

    def new_dma_sem(self, name):
        self.nsem = getattr(self, "nsem", 0) + 1
        d = [self.nc.alloc_semaphore("%s_u%d" % (name, self.nsem)), 0]
        self.dma_sems.append(d)
        return d

    def dma(self, queue, dsem, out, in_, reads=(), writes=(), **kw):
        self._deps(queue, reads, writes)
        ins = self.E[queue].dma_start(out=out, in_=in_, **kw)
        dsem[1] += 16
        ins.then_inc(dsem[0], 16)
        tok = (dsem[0], dsem[1])
        for b in reads:
            b.r.append(tok)
            if len(b.r) > 12:
                b.r = self._compact(b.r)
        for b in writes:
            b.w = tok
            b.r = []
        return ins

    def barrier(self):
        for k, e in self.E.items():
            wd = self.waited[k]
            for s, v in self.dma_sems:
                if v > 0 and wd.get(s, 0) < v:
                    e.wait_ge(s, v)
                    wd[s] = v
            for kk, s in self.sem.items():
                v = self.cnt[kk]
                if v > 0 and wd.get(s, 0) < v:
                    e.wait_ge(s, v)
                    wd[s] = v

    def finish(self):
        self.barrier()

    def rotate(self):
        self.barrier()
        for k in list(self.sem):
            self.sem[k] = self.nc.alloc_semaphore("s2_%s_%d" % (k, self.rr))
            self.cnt[k] = 0
        self.rr += 1


class Ctx:
    def __init__(self, nc):
        self.nc = nc
        self.S = Sched(nc)
        self.n = 0

    def alloc(self, stack, shape, dt, name=None):
        self.n += 1
        nm = (name or "t") + "_%d" % self.n
        t = stack.enter_context(self.nc.sbuf_tensor(nm, list(shape), dt))
        return t.ap(), Buf(nm)


def rsqrt_inplace(cx, t_ap, t_b, src_ap, src_b, mult, eng_mid="act"):
    nc, S = cx.nc, cx.S
    S.op("dve", lambda: nc.vector.tensor_scalar(out=t_ap, in0=src_ap, scalar1=mult, scalar2=EPS, op0=ALU.mult, op1=ALU.add), [src_b], [t_b])
    S.op("act", lambda: nc.scalar.activation(out=t_ap, in_=t_ap, func=AF.Sqrt), [t_b], [t_b])
    S.op("dve", lambda: nc.vector.reciprocal(out=t_ap, in_=t_ap), [t_b], [t_b])


def compute_mod_cols(cx, stack, ps, psb, c_d, wada_d, bada_d, ncols, ones_row, b_ones_row, one11, b_one11):
    nc, S = cx.nc, cx.S
    ccol, b_ccol = cx.alloc(stack, [128, NCH], F32, "ccol")
    dsm = S.new_dma_sem("d_mod")
    S.dma("sp", dsm, ccol, c_d, writes=[b_ccol])
    S.op("act", lambda: nc.scalar.activation(out=ccol, in_=ccol, func=AF.Silu), [b_ccol], [b_ccol])
    row, b_row = cx.alloc(stack, [1, ncols], F32, "modrow")
    brow, b_brow = cx.alloc(stack, [1, ncols], F32, "badarow")
    S.dma("sp", S.new_dma_sem("d_mod2"), brow, bada_d[:, 0:ncols], writes=[b_brow])
    wv = wada_d.rearrange("(kc p) n -> p kc n", p=128)
    with ExitStack() as st2:
        wt = [cx.alloc(st2, [128, NCH, 512], F32, "wadat") for _ in range(2)]
        dws = [S.new_dma_sem("d_wada%d" % i) for i in range(2)]
        for j in range(ncols // 512):
            w_ap, w_b = wt[j % 2]
            q = "sp" if j % 2 == 0 else "pool"
            S.dma(q, dws[j % 2], w_ap, wv[:, :, j * 512:(j + 1) * 512], writes=[w_b])
            for kc in range(NCH):
                S.op("pe", lambda: nc.tensor.matmul(ps[0][0:1, :], lhsT=ccol[:, kc:kc + 1], rhs=w_ap[:, kc, :], start=(kc == 0), stop=(kc == NCH - 1)),
                     [b_ccol, w_b], [psb[0]], signal=(kc == NCH - 1))
            S.op("dve", lambda: nc.vector.tensor_tensor(out=row[:, j * 512:(j + 1) * 512], in0=ps[0][0:1, :], in1=brow[:, j * 512:(j + 1) * 512], op=ALU.add),
                 [psb[0], b_brow], [b_row])
        S.barrier()
    return row, b_row


def row_to_cols(cx, ps_ap, ps_b, row_ap, row_b, n, one11, b_one11, dst_ap, dst_b):
    nc, S = cx.nc, cx.S
    for j in range(n):
        S.op("pe", lambda: nc.tensor.matmul(ps_ap[:, j:j + 1], lhsT=row_ap[0:1, j * 128:(j + 1) * 128], rhs=one11, start=True, stop=True),
             [row_b, b_one11], [ps_b], signal=(j == n - 1))
    S.op("dve", lambda: nc.vector.tensor_copy(out=dst_ap, in_=ps_ap[:, 0:n]), [ps_b], [dst_b])


def norm_transpose_tile(cx, x_src, xt, b_xt, dsem, junk, b_junk, ss, b_ss, xn, b_xn, ident, b_ident, ps, psb,
                        scol, b_scol, shcol, b_shcol, uT, b_uT, col0, queue="sp"):
    nc, S = cx.nc, cx.S
    S.dma(queue, dsem, xt, x_src, writes=[b_xt])
    S.op("act", lambda: nc.scalar.activation(out=junk, in_=xt, func=AF.Square, accum_out=ss), [b_xt], [b_junk, b_ss])
    rsqrt_inplace(cx, ss, b_ss, ss, b_ss, 1.0 / D)
    S.op("act", lambda: nc.scalar.activation(out=xn, in_=xt, func=AF.Identity, scale=ss), [b_xt, b_ss], [b_xn])
    for half in range(2):
        pT = ps[half].bitcast(BF16)
        for k in range(8):
            c = half * 8 + k
            S.op("pe", lambda: nc.tensor.transpose(out=pT[:, k * 128:(k + 1) * 128], in_=xn[:, c * 128:(c + 1) * 128], identity=ident),
                 [b_xn, b_ident], [psb[half]], signal=(k == 7))
        for k in range(8):
            c = half * 8 + k
            if k % 2 == 0:
                S.op("act", lambda: nc.scalar.activation(out=uT[:, c, col0:col0 + 128], in_=pT[:, k * 128:(k + 1) * 128], func=AF.Identity,
                                                         scale=scol[:, c:c + 1], bias=shcol[:, c:c + 1]),
                     [psb[half], b_scol, b_shcol], [b_uT])
            else:
                S.op("dve", lambda: nc.vector.tensor_scalar(out=uT[:, c, col0:col0 + 128], in0=pT[:, k * 128:(k + 1) * 128],
                                                            scalar1=scol[:, c:c + 1], scalar2=shcol[:, c:c + 1], op0=ALU.mult, op1=ALU.add),
                     [psb[half], b_scol, b_shcol], [b_uT])


NFM = 1168
NTM = 640


CW = 256


def build_l1(S_LEN, env=None):
    T = S_LEN // 128
    G = S_LEN // 512
    if env is None:
        nc = bass.Bass("TRN2", target_bir_lowering=False)
        x_d = nc.dram_tensor("x", [S_LEN, D], F32, kind="ExternalInput").ap()
        c_d = nc.dram_tensor("c", [128, NCH], F32, kind="ExternalInput").ap()
        wada_d = nc.dram_tensor("wada", [D, 2 * D], F32, kind="ExternalInput").ap()
        bada_d = nc.dram_tensor("bada", [1, 2 * D], F32, kind="ExternalInput").ap()
    else:
        nc = env["nc"]
        x_d, c_d, wada_d, bada_d = env["x"], env["c"], env["wada"], env["bada"]
    pos_d = nc.dram_tensor("pos", [1, S_LEN], I32, kind="ExternalInput").ap()
    g1_d = nc.dram_tensor("g1", [1, D], F32, kind="ExternalInput").ap()
    wfm_d = nc.dram_tensor("wfm", [D, NFM], F32, kind="ExternalInput").ap()
    wtm_d = nc.dram_tensor("wtm", [D, NTM], F32, kind="ExternalInput").ap()
    consts_d = nc.dram_tensor("consts", [128, 5, 128], F32, kind="ExternalInput").ap()
    fcol_d = nc.dram_tensor("fcol", [128, 1], F32, kind="ExternalInput").ap()
    qkg_d = nc.dram_tensor("qkg", [128, 2], F32, kind="ExternalInput").ap()
    lam_d = nc.dram_tensor("lam", [1, 4, 64], F32, kind="ExternalInput").ap()
    subg_d = nc.dram_tensor("subg", [128, 1], F32, kind="ExternalInput").ap()
    aup_d = nc.dram_tensor("aup", [32, 128], F32, kind="ExternalInput").ap()
    gout_d = nc.dram_tensor("gout", [128, 2], F32, kind="ExternalInput").ap()
    if env is None:
        yaT_d = nc.dram_tensor("yaT", [256, S_LEN], BF16, kind="ExternalOutput").ap()
        ybT_d = nc.dram_tensor("ybT", [256, S_LEN], BF16, kind="ExternalOutput").ap()
        ycat_fcw = None
    else:
        ycat_fcw = env["ycat"].rearrange("c f w -> f c w")
    NPG = 512 // CW
    qT_s = nc.dram_tensor("qT_s", [2, 128, S_LEN], BF16, kind="Internal").ap()
    kT_s = nc.dram_tensor("kT_s", [2, 128, S_LEN], BF16, kind="Internal").ap()
    v_s = nc.dram_tensor("v_s", [2, 128, T, 128], BF16, kind="Internal").ap()

    if env is None:
        cx = Ctx(nc)
        ps = [nc.alloc_psum_tensor("ps%d" % i, [128, 512], F32).ap() for i in range(8)]
        psb = [Buf("ps%d" % i, excl=True) for i in range(8)]
    else:
        cx, ps, psb = env["cx"], env["ps"], env["psb"]
    S = cx.S

    with ExitStack() as gst:
        cst, b_cst = cx.alloc(gst, [128, 5, 128], F32, "cst")
        d_c = S.new_dma_sem("d_c")
        S.dma("sp", S.new_dma_sem("d_cx0"), cst, consts_d, writes=[b_cst])
        identf, Ubd, Lbd, Rm, bones_f = [cst[:, i, :] for i in range(5)]
        ident, b_ident = cx.alloc(gst, [128, 128], BF16, "ident")
        ones_bf, b_ones = cx.alloc(gst, [128, 128], BF16, "ones")
        bones, b_bones = cx.alloc(gst, [128, 128], BF16, "bones")
        ones_f, b_ones_f = cx.alloc(gst, [128, 128], F32, "ones_f")
        S.op("dve", lambda: nc.vector.tensor_copy(out=ident, in_=identf), [b_cst], [b_ident])
        S.op("dve", lambda: nc.vector.tensor_copy(out=bones, in_=bones_f), [b_cst], [b_bones])
        S.op("pool", lambda: nc.gpsimd.memset(ones_bf, 1.0), [], [b_ones])
        S.op("pool", lambda: nc.gpsimd.memset(ones_f, 1.0), [], [b_ones_f])
        one11 = ones_f[0:1, 0:1]
        fcol, b_fcol = cx.alloc(gst, [128, 1], F32, "fcol")
        qkg, b_qkg = cx.alloc(gst, [128, 2], F32, "qkg")
        subg, b_subg = cx.alloc(gst, [128, 1], F32, "subg")
        gout, b_gout = cx.alloc(gst, [128, 2], F32, "gout")
        S.dma("sp", S.new_dma_sem("d_cx1"), fcol, fcol_d, writes=[b_fcol])
        S.dma("sp", S.new_dma_sem("d_cx2"), qkg, qkg_d, writes=[b_qkg])
        S.dma("sp", S.new_dma_sem("d_cx3"), subg, subg_d, writes=[b_subg])
        S.dma("sp", S.new_dma_sem("d_cx4"), gout, gout_d, writes=[b_gout])
        S.op("dve", lambda: nc.vector.tensor_scalar(out=qkg[:, 0:1], in0=qkg[:, 0:1], scalar1=0.125, scalar2=None, op0=ALU.mult), [b_qkg], [b_qkg])
        S.op("dve", lambda: nc.vector.tensor_scalar(out=subg, in0=subg, scalar1=0.8, scalar2=None, op0=ALU.mult), [b_subg], [b_subg])
        lam4, b_lam4 = cx.alloc(gst, [1, 4, 64], F32, "lam4")
        S.dma("sp", S.new_dma_sem("d_cx5"), lam4, lam_d, writes=[b_lam4])
        lp, b_lp = cx.alloc(gst, [1, 2, 64], F32, "lp")
        ls, b_ls = cx.alloc(gst, [1, 2], F32, "ls")
        S.op("dve", lambda: nc.vector.tensor_tensor(out=lp[:, 0, :], in0=lam4[:, 0, :], in1=lam4[:, 1, :], op=ALU.mult), [b_lam4], [b_lp])
        S.op("dve", lambda: nc.vector.tensor_tensor(out=lp[:, 1, :], in0=lam4[:, 2, :], in1=lam4[:, 3, :], op=ALU.mult), [b_lam4], [b_lp])
        S.op("dve", lambda: nc.vector.reduce_sum(out=ls, in_=lp, axis=AX.X), [b_lp], [b_ls])
        S.op("act", lambda: nc.scalar.activation(out=ls, in_=ls, func=AF.Exp), [b_ls], [b_ls])
        nl1, b_nl1 = cx.alloc(gst, [1, 1], F32, "nl1")
        S.op("dve", lambda: nc.vector.tensor_tensor(out=nl1, in0=ls[:, 1:2], in1=ls[:, 0:1], op=ALU.subtract), [b_ls], [b_nl1])
        S.op("dve", lambda: nc.vector.tensor_scalar(out=nl1, in0=nl1, scalar1=-0.2, scalar2=None, op0=ALU.add), [b_nl1], [b_nl1])
        negl, b_negl = cx.alloc(gst, [128, 1], F32, "negl")
        S.op("pe", lambda: nc.tensor.matmul(ps[7][:, 0:1], lhsT=ones_f[0:1, :], rhs=nl1, start=True, stop=True), [b_ones_f, b_nl1], [psb[7]])
        S.op("dve", lambda: nc.vector.tensor_copy(out=negl, in_=ps[7][:, 0:1]), [psb[7]], [b_negl])
        scol, b_scol = cx.alloc(gst, [128, NCH], F32, "scol")
        shcol, b_shcol = cx.alloc(gst, [128, NCH], F32, "shcol")
        with ExitStack() as st:
            row, b_row = compute_mod_cols(cx, st, ps, psb, c_d, wada_d, bada_d, 2 * D, None, None, one11, b_ones_f)
            g1r, b_g1r = cx.alloc(st, [1, D], F32, "g1r")
            S.dma("sp", S.new_dma_sem("d_cx6"), g1r, g1_d, writes=[b_g1r])
            S.op("dve", lambda: nc.vector.scalar_tensor_tensor(out=row[:, D:2 * D], in0=row[:, D:2 * D], scalar=1.0, in1=g1r, op0=ALU.add, op1=ALU.mult),
                 [b_row, b_g1r], [b_row])
            row_to_cols(cx, ps[1], psb[1], row[:, 0:D], b_row, NCH, one11, b_ones_f, shcol, b_shcol)
            row_to_cols(cx, ps[2], psb[2], row[:, D:2 * D], b_row, NCH, one11, b_ones_f, scol, b_scol)
            S.barrier()
        import os
        STOP = int(os.environ.get("L1_STOP", "9"))
        if STOP == 0:
            S.finish()
            return nc
        aupf, b_aupf = cx.alloc(gst, [32, 128], F32, "aupf")
        aup, b_aup = cx.alloc(gst, [32, 128], BF16, "aup")
        S.dma("sp", S.new_dma_sem("d_cx7"), aupf, aup_d, writes=[b_aupf])
        S.op("dve", lambda: nc.vector.tensor_copy(out=aup, in_=aupf), [b_aupf], [b_aup])

        with ExitStack() as st:
            wfm, b_wfm = cx.alloc(st, [128, NCH, NFM], BF16, "wfm")
            wtm, b_wtm = cx.alloc(st, [128, NCH, NTM], BF16, "wtm")
            with ExitStack() as st2:
                stg = [cx.alloc(st2, [128, NFM], F32, "wstg") for _ in range(2)]
                dws = [S.new_dma_sem("d_ws%d" % i) for i in range(2)]
                i = 0
                for (src, dst, b_dst, ncol) in ((wfm_d, wfm, b_wfm, NFM), (wtm_d, wtm, b_wtm, NTM)):
                    for kc in range(NCH):
                        sa, sb = stg[i % 2]
                        S.dma("sp" if i % 2 == 0 else "pool", dws[i % 2], sa[:, 0:ncol], src[kc * 128:(kc + 1) * 128, :], writes=[sb])
                        if i % 2 == 0:
                            S.op("dve", lambda: nc.vector.tensor_copy(out=dst[:, kc, :], in_=sa[:, 0:ncol]), [sb], [b_dst])
                        else:
                            S.op("act", lambda: nc.scalar.copy(out=dst[:, kc, :], in_=sa[:, 0:ncol]), [sb], [b_dst])
                        i += 1
                S.barrier()
            if STOP == 1:
                S.finish()
                return nc
            NXB = 3
            xts = [cx.alloc(st, [128, D], F32, "xt") for _ in range(NXB)]
            dxs = [S.new_dma_sem("d_x%d" % i) for i in range(NXB)]
            junk, b_junk = cx.alloc(st, [128, D], BF16, "junk")
            sss = [cx.alloc(st, [128, 1], F32, "ss") for _ in range(2)]
            xns = [cx.alloc(st, [128, D], BF16, "xn") for _ in range(2)]
            uTs = [cx.alloc(st, [128, NCH, 512], BF16, "uT") for _ in range(2)]
            posi, b_posi = cx.alloc(st, [128, 512], I32, "posi")
            posf, b_posf = cx.alloc(st, [128, 512], F32, "posf")
            ang, b_ang = cx.alloc(st, [128, 512], F32, "ang")
            sinT, b_sinT = cx.alloc(st, [128, 512], F32, "sinT")
            cosT, b_cosT = cx.alloc(st, [128, 512], F32, "cosT")
            d_pos = S.new_dma_sem("d_pos")
            rk, b_rk = cx.alloc(st, [128, 512], F32, "rk")
            rr, b_rr = cx.alloc(st, [128, 512], F32, "rr")
            qf, b_qf = cx.alloc(st, [128, 512], F32, "qf")
            sq, b_sq = cx.alloc(st, [128, 512], BF16, "sq")
            rt, b_rt = cx.alloc(st, [128, 512], F32, "rt")
            qn, b_qn = cx.alloc(st, [128, 512], F32, "qn")
            qa, b_qa = cx.alloc(st, [128, 512], F32, "qa")
            qb, b_qb = cx.alloc(st, [128, 512], F32, "qb")
            qos = [cx.alloc(st, [128, 512], BF16, "qo") for _ in range(2)]
            d_qo = [S.new_dma_sem("d_qo%d" % i) for i in range(2)]
            gqT, b_gqT = cx.alloc(st, [128, 512], BF16, "gqT")
            gkT, b_gkT = cx.alloc(st, [128, 512], BF16, "gkT")
            sr, b_sr = cx.alloc(st, [128, 2, 512], F32, "sr")
            alow, b_alow = cx.alloc(st, [32, 512], BF16, "alow")
            S.op("pool", lambda: nc.gpsimd.memset(alow, 1.0), [], [b_alow])
            vts = [cx.alloc(st, [128, 256], BF16, "vt") for _ in range(2)]
            d_vt = [S.new_dma_sem("d_vt%d" % i) for i in range(2)]
            gv, b_gv = cx.alloc(st, [128, 256], BF16, "gv")
            e1, b_e1 = cx.alloc(st, [128, 128], F32, "e1")
            lgl, b_lgl = cx.alloc(st, [128, 128], F32, "lgl")
            ebT, b_ebT = cx.alloc(st, [128, 128], F32, "ebT")
            enbT, b_enbT = cx.alloc(st, [128, 128], F32, "enbT")
            erb, b_erb = cx.alloc(st, [128, 128], F32, "erb")
            qtl, b_qtl = cx.alloc(st, [128, 128], BF16, "qtl")
            ktl, b_ktl = cx.alloc(st, [128, 128], BF16, "ktl")
            khat, b_khat = cx.alloc(st, [128, 128], BF16, "khat")
            gksb, b_gksb = cx.alloc(st, [128, 128], F32, "gksb")
            atsb, b_atsb = cx.alloc(st, [128, 128], F32, "atsb")
            atm, b_atm = cx.alloc(st, [128, 128], BF16, "atm")
            state, b_state = cx.alloc(st, [128, 256], F32, "state")
            state_bf, b_state_bf = cx.alloc(st, [128, 256], BF16, "state_bf")
            S.op("pool", lambda: nc.gpsimd.memset(state, 0.0), [], [b_state])
            S.op("pool", lambda: nc.gpsimd.memset(state_bf, 0.0), [], [b_state_bf])
            oI, b_oI = cx.alloc(st, [128, 2, 128], F32, "oI")
            osb, b_osb = cx.alloc(st, [128, 2, 128], F32, "osb")
            osq, b_osq = cx.alloc(st, [128, 2, 128], BF16, "osq")
            ors, b_ors = cx.alloc(st, [128, 128], F32, "ors")
            otmp, b_otmp = cx.alloc(st, [128, 128], F32, "otmp")
            ybs = [cx.alloc(st, [128, 2, 512], BF16, "yb") for _ in range(2)]
            d_yb = [S.new_dma_sem("d_yb%d" % i) for i in range(2)]
            p_fm, pb_fm = ps[2], psb[2]
            p_nr, pb_nr = ps[3], psb[3]
            p_tmA, pb_tmA = ps[4], psb[4]
            p5 = [ps[5][:, i * 128:(i + 1) * 128] for i in range(4)]
            pb5 = [psb[5]] * 4
            p_gk, p_z, p_cb, p_crb = p5
            import os as _os
            pb_gk, pb_z, pb_cb, pb_crb = pb5
            p_at, pb_at = ps[6][:, 0:128], psb[6]
            p_up, pb_up = ps[6][:, 128:384], psb[6]
            p_ss, pb_ss = ps[6][:, 384:512], psb[6]
            p_oT = [ps[7][:, i * 128:(i + 1) * 128] for i in range(2)]
            pb_oT = [psb[7]] * 2
            p_oI = [ps[7][:, (2 + i) * 128:(3 + i) * 128] for i in range(2)]
            pb_oI = [psb[7]] * 2

            xi = 0
            for g in range(G):
                uT, b_uT = uTs[g % 2]
                for tt in range(4):
                    t = g * 4 + tt
                    xt, b_xt = xts[xi % NXB]
                    ss, b_ss = sss[xi % 2]
                    xn, b_xn = xns[xi % 2]
                    norm_transpose_tile(cx, x_d[t * 128:(t + 1) * 128, :], xt, b_xt, dxs[xi % NXB], junk, b_junk, ss, b_ss, xn, b_xn,
                                        ident, b_ident, ps, psb, scol, b_scol, shcol, b_shcol, uT, b_uT, tt * 128,
                                        queue=("sp" if xi % 2 == 0 else "pool"))
                    xi += 1
                if STOP == 3:
                    S.finish()
                    return nc
                S.dma("sp", d_pos, posi, pos_d[0:1, g * 512:(g + 1) * 512].partition_broadcast(128), writes=[b_posi])
                S.op("dve", lambda: nc.vector.tensor_copy(out=posf, in_=posi), [b_posi], [b_posf])
                S.op("dve", lambda: nc.vector.tensor_scalar(out=ang, in0=posf, scalar1=fcol[:, 0:1], scalar2=None, op0=ALU.mult), [b_posf, b_fcol], [b_ang])
                for (dst, b_dst, off) in ((sinT, b_sinT, 0.0), (cosT, b_cosT, math.pi / 2)):
                    S.op("dve", lambda: nc.vector.tensor_scalar(out=rk, in0=ang, scalar1=1.0 / TWO_PI, scalar2=off / TWO_PI, op0=ALU.mult, op1=ALU.add),
                         [b_ang], [b_rk])
                    S.op("dve", lambda: nc.vector.tensor_scalar(out=rk, in0=rk, scalar1=MAGIC, scalar2=None, op0=ALU.add), [b_rk], [b_rk])
                    S.op("dve", lambda: nc.vector.tensor_scalar(out=rk, in0=rk, scalar1=-MAGIC, scalar2=None, op0=ALU.add), [b_rk], [b_rk])
                    S.op("dve", lambda: nc.vector.scalar_tensor_tensor(out=rr, in0=rk, scalar=-CW1, in1=ang, op0=ALU.mult, op1=ALU.add), [b_rk, b_ang], [b_rr])
                    S.op("dve", lambda: nc.vector.scalar_tensor_tensor(out=rr, in0=rk, scalar=-CW2, in1=rr, op0=ALU.mult, op1=ALU.add), [b_rk, b_rr], [b_rr])
                    S.op("act", lambda: nc.scalar.activation(out=dst, in_=rr, func=AF.Sin, bias=off), [b_rr], [b_dst])
                if STOP == 4:
                    S.finish()
                    return nc
                for j in range(9):
                    m = 16 if j == 8 else 128
                    po = p_fm[0:m, :]
                    for kc in range(NCH):
                        S.op("pe", lambda: nc.tensor.matmul(po, lhsT=wfm[:, kc, j * 128:j * 128 + m], rhs=uT[:, kc, :], start=(kc == 0), stop=(kc == NCH - 1)),
                             [b_wfm, b_uT], [pb_fm], signal=(kc == NCH - 1))
                    if j < 4:
                        isq = j < 2
                        h = j % 2
                        gcol = qkg[:, 0:1] if isq else qkg[:, 1:2]
                        S.op("act", lambda: nc.scalar.copy(out=qf, in_=po), [pb_fm], [b_qf])
                        S.op("act", lambda: nc.scalar.activation(out=sq, in_=po, func=AF.Square), [pb_fm], [b_sq])
                        S.op("pe", lambda: nc.tensor.matmul(p_nr, lhsT=bones, rhs=sq, start=True, stop=True), [b_bones, b_sq], [pb_nr])
                        rsqrt_inplace(cx, rt, b_rt, p_nr, pb_nr, 1.0 / 64)
                        S.op("dve", lambda: nc.vector.scalar_tensor_tensor(out=qn, in0=qf, scalar=gcol, in1=rt, op0=ALU.mult, op1=ALU.mult),
                             [b_qf, b_qkg, b_rt], [b_qn])
                        S.op("pe", lambda: nc.tensor.matmul(p_nr, lhsT=Rm, rhs=qn, start=True, stop=True), [b_cst, b_qn], [pb_nr])
                        S.op("pool", lambda: nc.gpsimd.tensor_tensor(out=qa, in0=qn, in1=cosT, op=ALU.mult), [b_qn, b_cosT], [b_qa])
                        S.op("dve", lambda: nc.vector.tensor_tensor(out=qb, in0=p_nr, in1=sinT, op=ALU.mult), [pb_nr, b_sinT], [b_qb])
                        qo, b_qo = qos[j % 2]
                        S.op("pool", lambda: nc.gpsimd.tensor_tensor(out=qo, in0=qa, in1=qb, op=ALU.add), [b_qa, b_qb], [b_qo])
                        dst = (qT_s if isq else kT_s)[h, :, g * 512:(g + 1) * 512]
                        S.dma("sp", d_qo[j % 2], dst, qo, reads=[b_qo])
                    elif j == 4:
                        S.op("act", lambda: nc.scalar.activation(out=gqT, in_=po, func=AF.Copy, scale=128.0 ** -0.5), [pb_fm], [b_gqT])
                    elif j == 5:
                        S.op("act", lambda: nc.scalar.copy(out=gkT, in_=po), [pb_fm], [b_gkT])
                    elif j in (6, 7):
                        S.op("act", lambda: nc.scalar.activation(out=sr[:, j - 6, :], in_=po, func=AF.Silu), [pb_fm], [b_sr])
                    else:
                        S.op("act", lambda: nc.scalar.copy(out=alow[0:16, :], in_=po), [pb_fm], [b_alow])
                if STOP == 5:
                    S.finish()
                    return nc
                yb, b_yb = ybs[g % 2]
                for tt in range(4):
                    t = g * 4 + tt
                    cs = slice(tt * 128, (tt + 1) * 128)
                    for kc in range(NCH):
                        S.op("pe", lambda: nc.tensor.matmul(p_tmA, lhsT=uT[:, kc, cs], rhs=wtm[:, kc, 0:512], start=(kc == 0), stop=(kc == NCH - 1)),
                             [b_uT, b_wtm], [pb_tmA], signal=(kc == NCH - 1))
                    for kc in range(NCH):
                        S.op("pe", lambda: nc.tensor.matmul(p_gk, lhsT=uT[:, kc, cs], rhs=wtm[:, kc, 512:640], start=(kc == 0), stop=(kc == NCH - 1)),
                             [b_uT, b_wtm], [pb_gk], signal=(kc == NCH - 1))
                    if STOP == 7:
                        S.finish()
                        return nc
                    vt, b_vt = vts[t % 2]
                    S.op("act", lambda: nc.scalar.copy(out=vt, in_=p_tmA[:, 0:256]), [pb_tmA], [b_vt])
                    for hh in range(2):
                        S.dma("sp", d_vt[t % 2], v_s[hh, :, t, :], vt[:, hh * 128:(hh + 1) * 128], reads=[b_vt])
                    if STOP == 8:
                        S.finish()
                        return nc
                    S.op("act", lambda: nc.scalar.copy(out=gv, in_=p_tmA[:, 256:512]), [pb_tmA], [b_gv])
                    if STOP == 6:
                        S.finish()
                        return nc
                    S.op("pe", lambda: nc.tensor.matmul(p_z, lhsT=alow[0:32, cs], rhs=aup, start=True, stop=True), [b_alow, b_aup], [pb_z])
                    S.op("act", lambda: nc.scalar.activation(out=e1, in_=p_z, func=AF.Exp, scale=-1.0), [pb_z], [b_e1])
                    S.op("act", lambda: nc.scalar.activation(out=lgl, in_=e1, func=AF.Ln, bias=1.0), [b_e1], [b_lgl])
                    if STOP == 11:
                        S.finish()
                        return nc
                    S.op("pe", lambda: nc.tensor.matmul(p_cb, lhsT=lgl, rhs=Ubd, start=True, stop=True), [b_lgl, b_cst], [pb_cb])
                    S.op("pe", lambda: nc.tensor.matmul(p_crb, lhsT=Lbd, rhs=lgl, start=True, stop=True), [b_lgl, b_cst], [pb_crb])
                    S.op("act", lambda: nc.scalar.activation(out=ebT, in_=p_cb, func=AF.Exp, scale=-1.0 / 16), [pb_cb], [b_ebT])
                    S.op("act", lambda: nc.scalar.activation(out=enbT, in_=p_cb, func=AF.Exp, scale=1.0 / 16), [pb_cb], [b_enbT])
                    S.op("act", lambda: nc.scalar.activation(out=erb, in_=p_crb, func=AF.Exp, scale=-1.0 / 16), [pb_crb], [b_erb])
                    if STOP == 12:
                        S.finish()
                        return nc
                    S.op("dve", lambda: nc.vector.tensor_tensor(out=qtl, in0=gqT[:, cs], in1=ebT, op=ALU.mult), [b_gqT, b_ebT], [b_qtl])
                    S.op("pool", lambda: nc.gpsimd.tensor_tensor(out=ktl, in0=gkT[:, cs], in1=enbT, op=ALU.mult), [b_gkT, b_enbT], [b_ktl])
                    S.op("act", lambda: nc.scalar.copy(out=gksb, in_=p_gk), [pb_gk], [b_gksb])
                    S.op("dve", lambda: nc.vector.tensor_tensor(out=khat, in0=gksb, in1=erb, op=ALU.mult), [b_gksb, b_erb], [b_khat])
                    S.op("pe", lambda: nc.tensor.matmul(p_at, lhsT=ktl, rhs=qtl, start=True, stop=True), [b_ktl, b_qtl], [pb_at])
                    S.op("act", lambda: nc.scalar.copy(out=atsb, in_=p_at), [pb_at], [b_atsb])
                    S.op("pool", lambda: nc.gpsimd.tensor_tensor(out=atm, in0=atsb, in1=Ubd, op=ALU.mult), [b_atsb, b_cst], [b_atm])
                    if STOP == 10:
                        S.finish()
                        return nc
                    for vh in range(2):
                        S.op("pe", lambda: nc.tensor.matmul(p_oT[vh], lhsT=gv[:, vh * 128:(vh + 1) * 128], rhs=atm, start=True, stop=True),
                             [b_gv, b_atm], [pb_oT[vh]])
                    for ch in range(2):
                        c0 = ch * 64
                        for vh in range(2):
                            S.op("pe", lambda: nc.tensor.matmul(p_oI[vh][:, c0:c0 + 64], lhsT=state_bf[:, vh * 128:(vh + 1) * 128], rhs=qtl[:, c0:c0 + 64],
                                                                start=True, stop=True), [b_state_bf, b_qtl], [pb_oI[vh]])
                        S.op("pe", lambda: nc.tensor.matmul(p_up, lhsT=khat[c0:c0 + 64, :], rhs=gv[c0:c0 + 64, :], start=True, stop=True),
                             [b_khat, b_gv], [pb_up])
                        S.op("dve", lambda: nc.vector.scalar_tensor_tensor(out=state, in0=state, scalar=ebT[:, c0 + 63:c0 + 64], in1=p_up, op0=ALU.mult, op1=ALU.add),
                             [b_state, b_ebT, pb_up], [b_state])
                        S.op("act", lambda: nc.scalar.copy(out=state_bf, in_=state), [b_state], [b_state_bf])
                    for vh in range(2):
                        S.op("act", lambda: nc.scalar.copy(out=oI[:, vh, :], in_=p_oI[vh]), [pb_oI[vh]], [b_oI])
                        S.op("dve", lambda: nc.vector.tensor_tensor(out=osb[:, vh, :], in0=p_oT[vh], in1=oI[:, vh, :], op=ALU.add), [pb_oT[vh], b_oI], [b_osb])
                    S.op("act", lambda: nc.scalar.activation(out=osq, in_=osb, func=AF.Square), [b_osb], [b_osq])
                    for vh in range(2):
                        S.op("pe", lambda: nc.tensor.matmul(p_ss, lhsT=ones_bf, rhs=osq[:, vh, :], start=(vh == 0), stop=(vh == 1)), [b_ones, b_osq], [pb_ss],
                             signal=(vh == 1))
                    rsqrt_inplace(cx, ors, b_ors, p_ss, pb_ss, 1.0 / 256)
                    for vh in range(2):
                        S.op("dve", lambda: nc.vector.scalar_tensor_tensor(out=otmp, in0=osb[:, vh, :], scalar=gout[:, vh:vh + 1], in1=ors, op0=ALU.mult, op1=ALU.mult),
                             [b_osb, b_gout, b_ors], [b_otmp])
                        S.op("pool", lambda: nc.gpsimd.tensor_tensor(out=yb[:, vh, cs], in0=otmp, in1=sr[:, vh, cs], op=ALU.mult), [b_otmp, b_sr], [b_yb])
                if ycat_fcw is None:
                    S.dma("pool", d_yb[g % 2], ybT_d[:, g * 512:(g + 1) * 512].rearrange("(v p) n -> p v n", p=128), yb, reads=[b_yb])
                else:
                    for v in range(2):
                        S.dma("pool", d_yb[g % 2], ycat_fcw[256 + v * 128:256 + (v + 1) * 128, g * NPG:(g + 1) * NPG, :],
                              yb[:, v, :].rearrange("p (c w) -> p c w", w=CW), reads=[b_yb])
            S.barrier()

        if STOP == 2:
            S.finish()
            return nc
        S.rotate()
        with ExitStack() as st:
            qT, b_qT = cx.alloc(st, [128, S_LEN], BF16, "qT")
            qT1, b_qT1 = cx.alloc(st, [128, S_LEN], BF16, "qT1")
            S.op("pool", lambda: nc.gpsimd.memset(qT, 0.0), [], [b_qT])
            S.op("pool", lambda: nc.gpsimd.memset(qT1, 0.0), [], [b_qT1])
            kT, b_kT = cx.alloc(st, [128, S_LEN], BF16, "kT")
            vv, b_vv = cx.alloc(st, [128, T, 128], BF16, "vv")
            d_ld4 = [S.new_dma_sem("d_ld%d" % i) for i in range(4)]
            pts = [[cx.alloc(st, [128, 512], BF16, "pt") for _ in range(2)] for _ in range(2)]
            rl, b_rl = cx.alloc(st, [128, 2, 512], F32, "rl")
            t01, b_t01 = cx.alloc(st, [128, 2, 512], F32, "t01")
            of, b_of = cx.alloc(st, [128, 512], F32, "of")
            osq2, b_osq2 = cx.alloc(st, [128, 512], BF16, "osq2")
            rs2, b_rs2 = cx.alloc(st, [128, 512], F32, "rs2")
            yas = [cx.alloc(st, [128, 512], BF16, "ya") for _ in range(2)]
            d_ya = [S.new_dma_sem("d_ya%d" % i) for i in range(2)]
            pS = [[ps[0], ps[1]], [ps[2], ps[3]]]
            pSb = [[psb[0], psb[1]], [psb[2], psb[3]]]
            pO, pOb = [ps[4], ps[5]], [psb[4], psb[5]]
            pL, pLb = [ps[6], ps[7]], [psb[6], psb[7]]
            it = 0
            for h in range(2):
                nsp = max(1, S_LEN // 4096)
                for i in range(nsp):
                    sl = slice(i * (S_LEN // nsp), (i + 1) * (S_LEN // nsp))
                    S.dma("sp", d_ld4[0], qT[0:64, sl], qT_s[h, 0:64, sl], writes=[b_qT])
                    S.dma("sp", d_ld4[1], qT1[64:128, sl], qT_s[h, 64:128, sl], writes=[b_qT1])
                    S.dma("pool", d_ld4[2], kT[:, sl], kT_s[h, :, sl], writes=[b_kT])
                    tl = slice(i * (T // nsp), (i + 1) * (T // nsp))
                    S.dma("sp", d_ld4[3], vv[:, tl, :], v_s[h, :, tl, :], writes=[b_vv])
                for g in range(G):
                    nk = 4 * g + 4
                    qs = slice(g * 512, (g + 1) * 512)
                    for kt in range(nk):
                        ks = slice(kt * 128, (kt + 1) * 128)
                        for c in range(2):
                            rows = slice(64 * c, 64 * c + 64)
                            pSc, pScb = pS[it % 2][c], pSb[it % 2][c]
                            pt, b_pt = pts[it % 2][c]
                            qsel = qT if c == 0 else qT1
                            S.op("pe", lambda: nc.tensor.matmul(pSc, lhsT=kT[:, ks], rhs=qsel[:, qs], start=True, stop=True), [b_kT, b_qT, b_qT1], [pScb])
                            S.op("act", lambda: nc.scalar.activation(out=pt, in_=pSc, func=AF.Exp), [pScb], [b_pt])
                            if kt >= 4 * g:
                                S.op("pool", lambda: nc.gpsimd.affine_select(out=pt, in_=pt, pattern=[[1, 512]], compare_op=ALU.is_ge, fill=0.0,
                                                                            base=g * 512 - kt * 128, channel_multiplier=-1), [b_pt], [b_pt])
                            S.op("pe", lambda: nc.tensor.matmul(pO[c], lhsT=vv[:, kt, :], rhs=pt, start=(kt == 0), stop=(kt == nk - 1)),
                                 [b_vv, b_pt], [pOb[c], pLb[c]], signal=False)
                            S.op("pe", lambda: nc.tensor.matmul(pL[c], lhsT=ones_bf, rhs=pt, start=(kt == 0), stop=(kt == nk - 1)),
                                 [b_vv, b_pt, b_ones], [pOb[c], pLb[c]])
                        it += 1
                    for c in range(2):
                        S.op("dve", lambda: nc.vector.reciprocal(out=rl[:, c, :], in_=pL[c]), [pLb[c]], [b_rl])
                        S.op("dve", lambda: nc.vector.tensor_tensor(out=t01[:, c, :], in0=pO[c], in1=rl[:, c, :], op=ALU.mult), [pOb[c], b_rl], [b_t01])
                    S.op("dve", lambda: nc.vector.scalar_tensor_tensor(out=of, in0=t01[:, 1, :], scalar=negl[:, 0:1], in1=t01[:, 0, :], op0=ALU.mult, op1=ALU.add),
                         [b_t01, b_negl], [b_of])
                    S.op("act", lambda: nc.scalar.activation(out=osq2, in_=of, func=AF.Square), [b_of], [b_osq2])
                    pE, pEb = pS[it % 2][0], pSb[it % 2][0]
                    S.op("pe", lambda: nc.tensor.matmul(pE, lhsT=ones_bf, rhs=osq2, start=True, stop=True), [b_ones, b_osq2], [pEb])
                    rsqrt_inplace(cx, rs2, b_rs2, pE, pEb, 1.0 / 128)
                    ya, b_ya = yas[g % 2]
                    S.op("dve", lambda: nc.vector.scalar_tensor_tensor(out=ya, in0=of, scalar=subg[:, 0:1], in1=rs2, op0=ALU.mult, op1=ALU.mult),
                         [b_of, b_subg, b_rs2], [b_ya])
                    if ycat_fcw is None:
                        S.dma("sp", d_ya[g % 2], yaT_d[h * 128:(h + 1) * 128, qs], ya, reads=[b_ya])
                    else:
                        S.dma("sp", d_ya[g % 2], ycat_fcw[h * 128:(h + 1) * 128, g * NPG:(g + 1) * NPG, :],
                              ya.rearrange("p (c w) -> p c w", w=CW), reads=[b_ya])
            S.barrier()
        if env is None:
            S.finish()
        else:
            S.rotate()
    return nc


def host_consts():
    ident = np.eye(128, dtype=np.float32)
    U = np.zeros((128, 128), np.float32)
    L = np.zeros((128, 128), np.float32)
    for a in range(128):
        for b in range(128):
            if a // 64 == b // 64:
                if a <= b:
                    U[a, b] = 1.0
                if a > b:
                    L[a, b] = 1.0
    R = np.zeros((128, 128), np.float32)
    for cc in range(2):
        for d in range(8):
            R[64 * cc + d + 8, 64 * cc + d] = -1.0
            R[64 * cc + d, 64 * cc + d + 8] = 1.0
    bones = np.zeros((128, 128), np.float32)
    bones[0:64, 0:64] = 1.0
    bones[64:128, 64:128] = 1.0
    consts = np.stack([ident, U, L, R, bones], axis=1).astype(np.float32)
    inv_freq = (500000.0 ** (-(np.arange(0, 16, 2, dtype=np.float32)) / 16.0)).astype(np.float32)
    fcol = np.zeros((128, 1), np.float32)
    for r in range(128):
        d = r % 64
        if d < 16:
            fcol[r, 0] = inv_freq[d % 8]
    return consts, fcol


def l1_inputs(inp, S_LEN):
    consts, fcol = host_consts()
    w_in = inp["w_in"][0]
    o_daq, o_dak, o_dav = 0, 1024, 2048
    o_gq, o_gk, o_gv, o_gr, o_ga = 3072, 3584, 4096, 5120, 6144
    maps = []
    for core in range(8):
        b, hg = core // 4, core % 4
        sl256 = slice(256 * hg, 256 * hg + 256)
        sl128 = slice(128 * hg, 128 * hg + 128)
        gk_w = w_in[:, o_gk:o_gk + 512][:, sl128]
        wfm = np.concatenate([w_in[:, o_daq:o_daq + 1024][:, sl256], w_in[:, o_dak:o_dak + 1024][:, sl256],
                              w_in[:, o_gq:o_gq + 512][:, sl128], gk_w, w_in[:, o_gr:o_gr + 1024][:, sl256],
                              w_in[:, o_ga:o_ga + 16], np.zeros((D, NFM - 1040), np.float32)], axis=1)
        wtm = np.concatenate([w_in[:, o_dav:o_dav + 1024][:, sl256], w_in[:, o_gv:o_gv + 1024][:, sl256], gk_w], axis=1)
        qkg = np.stack([np.tile(inp["da_q_norm_g"][0], 2), np.tile(inp["da_k_norm_g"][0], 2)], axis=1)
        lam = np.stack([inp["da_lambda_q1"][0], inp["da_lambda_k1"][0], inp["da_lambda_q2"][0], inp["da_lambda_k2"][0]], 0)[None]
        aup = np.concatenate([inp["gla_gate_up"][0][:, sl128], inp["gla_gate_bias"][0][None, sl128], np.zeros((15, 128), np.float32)], axis=0)
        gout = inp["gla_out_norm_g"][0].reshape(2, 128).T
        maps.append({
            "x": np.ascontiguousarray(inp["x"][b, :S_LEN]),
            "pos": np.ascontiguousarray(inp["positions"][b:b + 1, :S_LEN]).astype(np.int32),
            "c": np.ascontiguousarray(inp["c"][b].reshape(NCH, 128).T),
            "wada": np.ascontiguousarray(inp["w_ada"][0][:, 0:2 * D]),
            "bada": np.ascontiguousarray(inp["b_ada"][0][None, 0:2 * D]),
            "g1": np.ascontiguousarray(inp["norm1_g"][0][None]),
            "wfm": np.ascontiguousarray(wfm, dtype=np.float32),
            "wtm": np.ascontiguousarray(wtm, dtype=np.float32),
            "consts": consts, "fcol": fcol,
            "qkg": np.ascontiguousarray(qkg, dtype=np.float32),
            "lam": np.ascontiguousarray(lam, dtype=np.float32),
            "subg": np.ascontiguousarray(inp["da_subln_g"][0][:, None]),
            "aup": np.ascontiguousarray(aup, dtype=np.float32),
            "gout": np.ascontiguousarray(gout, dtype=np.float32),
        })
    return maps


def run_l1(inp, S_LEN):
    nc = build_l1(S_LEN)
    res = run_bass_kernel_spmd(nc, l1_inputs(inp, S_LEN), core_ids=list(range(8)))
    yaT = np.zeros((2, 1024, S_LEN), ml_dtypes.bfloat16)
    ybT = np.zeros((2, 1024, S_LEN), ml_dtypes.bfloat16)
    for core in range(8):
        b, hg = core // 4, core % 4
        yaT[b, 256 * hg:256 * hg + 256] = res.results[core]["yaT"]
        ybT[b, 256 * hg:256 * hg + 256] = res.results[core]["ybT"]
    return yaT, ybT


def build_l2(N, env=None):
    NG = N // 512
    if env is None:
        nc = bass.Bass("TRN2", target_bir_lowering=False)
        x_d = nc.dram_tensor("x", [N, D], F32, kind="ExternalInput").ap()
        yaT_d = nc.dram_tensor("yaT", [1024, N], BF16, kind="ExternalInput").ap()
        ybT_d = nc.dram_tensor("ybT", [1024, N], BF16, kind="ExternalInput").ap()
        c_d = nc.dram_tensor("c", [128, NCH], F32, kind="ExternalInput").ap()
        wada_d = nc.dram_tensor("wada", [D, 6 * D], F32, kind="ExternalInput").ap()
        bada_d = nc.dram_tensor("bada", [1, 6 * D], F32, kind="ExternalInput").ap()
    else:
        nc = env["nc"]
        x_d = nc.dram_tensor("l2_x", [N, D], F32, kind="ExternalInput").ap()
        c_d, wada_d, bada_d = env["c"], env["wada"], env["bada"]
    g12_d = nc.dram_tensor("g12", [1, 2 * D], F32, kind="ExternalInput").ap()
    wg_d = nc.dram_tensor("wg", [D, 2 * D], F32, kind="ExternalInput").ap()
    wpa_d = nc.dram_tensor("wpa", [1024, D], F32, kind="ExternalInput").ap()
    wpb_d = nc.dram_tensor("wpb", [1024, D], F32, kind="ExternalInput").ap()
    wo_d = nc.dram_tensor("wo", [D, D], F32, kind="ExternalInput").ap()
    w1_d = nc.dram_tensor("w1", [D, 4 * D], F32, kind="ExternalInput").ap()
    w2_d = nc.dram_tensor("w2", [4 * D, D], F32, kind="ExternalInput").ap()
    ident_d = nc.dram_tensor("ident", [128, 128], F32, kind="ExternalInput").ap()
    out_d = nc.dram_tensor("out", [N, D], F32, kind="ExternalOutput").ap()

    if env is None:
        cx = Ctx(nc)
        ps = [nc.alloc_psum_tensor("ps%d" % i, [128, 512], F32).ap() for i in range(8)]
        psb = [Buf("ps%d" % i, excl=True) for i in range(8)]
    else:
        cx, ps, psb = env["cx"], env["ps"], env["psb"]
    S = cx.S
    with ExitStack() as gst:
        identf, b_identf = cx.alloc(gst, [128, 128], F32, "identf")
        ident, b_ident = cx.alloc(gst, [128, 128], BF16, "ident")
        ones_f, b_ones_f = cx.alloc(gst, [128, 128], F32, "ones_f")
        d_c = S.new_dma_sem("d_c")
        S.dma("sp", S.new_dma_sem("d_cx8"), identf, ident_d, writes=[b_identf])
        S.op("dve", lambda: nc.vector.tensor_copy(out=ident, in_=identf), [b_identf], [b_ident])
        S.op("pool", lambda: nc.gpsimd.memset(ones_f, 1.0), [], [b_ones_f])
        one11 = ones_f[0:1, 0:1]
        cols, b_cols = cx.alloc(gst, [128, 6, NCH], F32, "cols")
        with ExitStack() as st:
            row, b_row = compute_mod_cols(cx, st, ps, psb, c_d, wada_d, bada_d, 6 * D, None, None, one11, b_ones_f)
            g12, b_g12 = cx.alloc(st, [1, 2 * D], F32, "g12")
            S.dma("sp", S.new_dma_sem("d_cx9"), g12, g12_d, writes=[b_g12])
            for (ci, gi) in ((1, 0), (4, 1)):
                S.op("dve", lambda: nc.vector.scalar_tensor_tensor(out=row[:, ci * D:(ci + 1) * D], in0=row[:, ci * D:(ci + 1) * D], scalar=1.0,
                                                                   in1=g12[:, gi * D:(gi + 1) * D], op0=ALU.add, op1=ALU.mult), [b_row, b_g12], [b_row])
            for ci in range(6):
                row_to_cols(cx, ps[1 + ci % 2], psb[1 + ci % 2], row[:, ci * D:(ci + 1) * D], b_row, NCH, one11, b_ones_f, cols[:, ci, :], b_cols)
            S.barrier()
        sh1, sc1, gt1, sh2, sc2, gt2 = [cols[:, i, :] for i in range(6)]

        xs, b_xs_all = cx.alloc(gst, [128, 4, D], F32, "xs")
        b_xs = [Buf("xs%d" % i) for i in range(4)]
        d_xs = [S.new_dma_sem("d_xs%d" % i) for i in range(4)]
        d_out = [S.new_dma_sem("d_out%d" % i) for i in range(4)]
        junk, b_junk = cx.alloc(gst, [128, D], BF16, "junk")
        sss = [cx.alloc(gst, [128, 1], F32, "ss") for _ in range(2)]
        xns = [cx.alloc(gst, [128, D], BF16, "xn") for _ in range(2)]
        uT, b_uT = cx.alloc(gst, [128, NCH, 512], BF16, "uT")
        hid, b_hid = cx.alloc(gst, [128, 64, 512], BF16, "hid")
        mergedT = hid[:, 0:16, :]
        yT = [hid[:, 16:24, :], hid[:, 24:32, :]]
        d_y = S.new_dma_sem("d_y")
        NW = 3
        stg = [cx.alloc(gst, [128, NCH, 128], F32, "stg") for _ in range(NW)]
        wbf = [cx.alloc(gst, [128, NCH, 128], BF16, "wbf") for _ in range(NW)]
        d_w = [S.new_dma_sem("d_w%d" % i) for i in range(NW)]
        sg = [cx.alloc(gst, [128, 512], F32, "sg") for _ in range(2)]
        tmpa, b_tmpa = cx.alloc(gst, [128, 512], F32, "tmpa")
        moTs = [cx.alloc(gst, [128, 512], F32, "moT") for _ in range(2)]
        rl, b_rl = cx.alloc(gst, [128, 512], F32, "rl")
        wi = [0]

        def stream(src, kc_n):
            i = wi[0] % NW
            wi[0] += 1
            sa, sb = stg[i]
            wa, wb = wbf[i]
            q = "sp" if wi[0] % 2 == 0 else "pool"
            S.dma(q, d_w[i], sa[:, 0:kc_n, :], src.rearrange("(kc p) m -> p kc m", p=128), writes=[sb])
            if wi[0] % 2 == 0:
                S.op("dve", lambda: nc.vector.tensor_copy(out=wa[:, 0:kc_n, :], in_=sa[:, 0:kc_n, :]), [sb], [wb])
            else:
                S.op("pool", lambda: nc.gpsimd.tensor_copy(out=wa[:, 0:kc_n, :], in_=sa[:, 0:kc_n, :]), [sb], [wb])
            return wa, wb

        acc_i = [0]

        def fm_matmul(w_ap, w_b, rhs_fn, rhs_b, kc_n, pacc=None, first=True, last=True):
            if pacc is None:
                k = 2 + acc_i[0] % 2
                acc_i[0] += 1
                pacc = (ps[k], psb[k])
            for kc in range(kc_n):
                S.op("pe", lambda: nc.tensor.matmul(pacc[0], lhsT=w_ap[:, kc, :], rhs=rhs_fn(kc), start=(first and kc == 0), stop=(last and kc == kc_n - 1)),
                     [w_b, rhs_b], [pacc[1]], signal=(last and kc == kc_n - 1))
            return pacc

        tr_i = [0]

        def gate_transpose_add(pacc, gcol, mc):
            mo, b_mo = moTs[tr_i[0] % 2]
            pT4, pT4b = ps[4 + tr_i[0] % 2], psb[4 + tr_i[0] % 2]
            tr_i[0] += 1
            S.op("act", lambda: nc.scalar.activation(out=mo, in_=pacc[0], func=AF.Copy, scale=gcol[:, mc:mc + 1]), [pacc[1], b_cols], [b_mo])
            for tt in range(4):
                S.op("pe", lambda: nc.tensor.transpose(out=pT4[:, tt * 128:(tt + 1) * 128], in_=mo[:, tt * 128:(tt + 1) * 128], identity=identf),
                     [b_mo, b_identf], [pT4b], signal=(tt == 3))
            for tt in range(4):
                S.op("dve", lambda: nc.vector.tensor_tensor(out=xs[:, tt, mc * 128:(mc + 1) * 128], in0=pT4[:, tt * 128:(tt + 1) * 128],
                                                            in1=xs[:, tt, mc * 128:(mc + 1) * 128], op=ALU.add), [pT4b, b_xs[tt]], [b_xs[tt]])

        def norm_T(tt, xi, scol, shcol):
            ss, b_ss = sss[xi % 2]
            xn, b_xn = xns[xi % 2]
            xt = xs[:, tt, :]
            S.op("act", lambda: nc.scalar.activation(out=junk, in_=xt, func=AF.Square, accum_out=ss), [b_xs[tt]], [b_junk, b_ss])
            rsqrt_inplace(cx, ss, b_ss, ss, b_ss, 1.0 / D)
            S.op("act", lambda: nc.scalar.activation(out=xn, in_=xt, func=AF.Identity, scale=ss), [b_xs[tt], b_ss], [b_xn])
            for half in range(2):
                pT = ps[half].bitcast(BF16)
                for k in range(8):
                    c = half * 8 + k
                    S.op("pe", lambda: nc.tensor.transpose(out=pT[:, k * 128:(k + 1) * 128], in_=xn[:, c * 128:(c + 1) * 128], identity=ident),
                         [b_xn, b_ident], [psb[half]], signal=(k == 7))
                for k in range(8):
                    c = half * 8 + k
                    S.op("act", lambda: nc.scalar.activation(out=uT[:, c, tt * 128:(tt + 1) * 128], in_=pT[:, k * 128:(k + 1) * 128], func=AF.Identity,
                                                             scale=scol[:, c:c + 1], bias=shcol[:, c:c + 1]), [psb[half], b_cols], [b_uT])

        xi = 0
        for g in range(NG):
            r0 = g * 512
            for tt in range(4):
                S.dma("sp" if tt % 2 == 0 else "pool", d_xs[tt], xs[:, tt, :], x_d[r0 + tt * 128:r0 + (tt + 1) * 128, :], writes=[b_xs[tt]])
                norm_T(tt, xi, sc1, sh1)
                xi += 1
            if env is None:
                for w, src in enumerate((yaT_d, ybT_d)):
                    S.dma("sp", d_y, yT[w], src[:, r0:r0 + 512].rearrange("(kc p) n -> p kc n", p=128), writes=[b_hid])
            else:
                for w in range(2):
                    S.dma("sp", d_y, yT[w], env["mine"][w].rearrange("r h p n -> p (r h) n")[:, :, r0:r0 + 512],
                          reads=[env["b_mine"]], writes=[b_hid])
            for mc in range(NCH):
                ms = slice(mc * 128, (mc + 1) * 128)
                for w, (wp_d, goff) in enumerate(((wpa_d, 0), (wpb_d, D))):
                    wa, wb = stream(wg_d[:, goff + mc * 128:goff + (mc + 1) * 128], NCH)
                    pg = fm_matmul(wa, wb, lambda kc: uT[:, kc, :], b_uT, NCH)
                    sga, b_sga = sg[w]
                    S.op("act", lambda: nc.scalar.activation(out=sga, in_=pg[0], func=AF.Sigmoid), [pg[1]], [b_sga])
                    wa, wb = stream(wp_d[:, ms], 8)
                    pp = fm_matmul(wa, wb, lambda kc: yT[w][:, kc, :], b_hid, 8)
                    if w == 0:
                        S.op("dve", lambda: nc.vector.tensor_tensor(out=tmpa, in0=pp[0], in1=sga, op=ALU.mult), [pp[1], b_sga], [b_tmpa])
                    else:
                        S.op("dve", lambda: nc.vector.tensor_tensor(out=rl, in0=pp[0], in1=sga, op=ALU.mult), [pp[1], b_sga], [b_rl])
                        S.op("pool", lambda: nc.gpsimd.tensor_tensor(out=mergedT[:, mc, :], in0=rl, in1=tmpa, op=ALU.add), [b_rl, b_tmpa], [b_hid])
            for mc in range(NCH):
                wa, wb = stream(wo_d[:, mc * 128:(mc + 1) * 128], NCH)
                pm = fm_matmul(wa, wb, lambda kc: mergedT[:, kc, :], b_hid, NCH)
                gate_transpose_add(pm, gt1, mc)
            for tt in range(4):
                norm_T(tt, xi, sc2, sh2)
                xi += 1
            for fc in range(64):
                wa, wb = stream(w1_d[:, fc * 128:(fc + 1) * 128], NCH)
                ph = fm_matmul(wa, wb, lambda kc: uT[:, kc, :], b_uT, NCH)
                S.op("act", lambda: nc.scalar.activation(out=rl, in_=ph[0], func=AF.Relu), [ph[1]], [b_rl])
                S.op("pool", lambda: nc.gpsimd.tensor_tensor(out=hid[:, fc, :], in0=rl, in1=rl, op=ALU.mult), [b_rl], [b_hid])
            for mc in range(NCH):
                k = 2 + acc_i[0] % 2
                acc_i[0] += 1
                pacc = (ps[k], psb[k])
                for q4 in range(4):
                    wa, wb = stream(w2_d[q4 * 2048:(q4 + 1) * 2048, mc * 128:(mc + 1) * 128], NCH)
                    fm_matmul(wa, wb, lambda kc: hid[:, q4 * 16 + kc, :], b_hid, NCH, pacc=pacc, first=(q4 == 0), last=(q4 == 3))
                gate_transpose_add(pacc, gt2, mc)
            for tt in range(4):
                S.dma("sp" if tt % 2 == 0 else "pool", d_out[tt], out_d[r0 + tt * 128:r0 + (tt + 1) * 128, :], xs[:, tt, :], reads=[b_xs[tt]])
        S.finish()
    return nc


def build_fused(S_LEN):
    N = S_LEN // 4
    nc = bass.Bass("TRN2", target_bir_lowering=False)
    cx = Ctx(nc)
    S = cx.S
    env = {"nc": nc, "cx": cx}
    env["ps"] = [nc.alloc_psum_tensor("ps%d" % i, [128, 512], F32).ap() for i in range(8)]
    env["psb"] = [Buf("ps%d" % i, excl=True) for i in range(8)]
    env["x"] = nc.dram_tensor("x", [S_LEN, D], F32, kind="ExternalInput").ap()
    env["c"] = nc.dram_tensor("c", [128, NCH], F32, kind="ExternalInput").ap()
    env["wada"] = nc.dram_tensor("wada", [D, 6 * D], F32, kind="ExternalInput").ap()
    env["bada"] = nc.dram_tensor("bada", [1, 6 * D], F32, kind="ExternalInput").ap()
    NCK = S_LEN // CW
    ycat_t = nc.dram_tensor("ycat", [NCK, 512, CW], BF16)
    gath_t = nc.dram_tensor("gath", [NCK, 4 * 512, CW], BF16)
    env["ycat"] = ycat_t.ap()
    env["gath"] = gath_t.ap()
    build_l1(S_LEN, env=env)
    cc_sem = nc.alloc_semaphore("cc_sem")
    g = nc.gpsimd
    for i in range(NCK):
        g.collective_compute("AllGather", ALU.bypass, replica_groups=[[0, 1, 2, 3], [4, 5, 6, 7]],
                             ins=[ycat_t.ap()[i].opt()], outs=[gath_t.ap()[i].opt()]).then_inc(cc_sem)
        g.wait_ge(cc_sem, i + 1)
    b_gath = Buf("gath")
    b_gath.w = (cc_sem, NCK)
    S.waited["pool"][cc_sem] = NCK
    mine = nc.dram_tensor("mine", [2, 4, 2, 128, N], BF16).ap()
    me = g.partition_id() % 4
    gv = gath_t.ap().rearrange("(j cc) (r w h p) s -> r w h j p cc s", j=4, r=4, w=2, h=2, p=128)
    d_mine = S.new_dma_sem("d_mine")
    b_mine = Buf("mine")
    for w in range(2):
        for r in range(4):
            for hh in range(2):
                S.dma("pool", d_mine, mine[w, r, hh].rearrange("p (cc s) -> p cc s", s=CW), gv[r, w, hh][bass.ds(me, 1)][0],
                      reads=[b_gath], writes=[b_mine])
    env["mine"] = mine
    env["b_mine"] = b_mine
    build_l2(N, env=env)
    return nc


def l2_inputs(inp, yaT, ybT, N, S_LEN):
    ident = np.eye(128, dtype=np.float32)
    w_in = inp["w_in"][0]
    wg = np.ascontiguousarray(w_in[:, 6160:6160 + 2 * D])
    maps = []
    for core in range(8):
        b, j = core // 4, core % 4
        sl = slice(j * N, (j + 1) * N)
        maps.append({
            "x": np.ascontiguousarray(inp["x"][b, :S_LEN][sl]),
            "yaT": np.ascontiguousarray(yaT[b][:, sl]),
            "ybT": np.ascontiguousarray(ybT[b][:, sl]),
            "c": np.ascontiguousarray(inp["c"][b].reshape(NCH, 128).T),
            "wada": np.ascontiguousarray(inp["w_ada"][0]),
            "bada": np.ascontiguousarray(inp["b_ada"][0][None]),
            "g12": np.ascontiguousarray(np.concatenate([inp["norm1_g"][0], inp["norm2_g"][0]])[None]),
            "wg": wg,
            "wpa": np.ascontiguousarray(inp["w_branch_a"][0]),
            "wpb": np.ascontiguousarray(inp["w_branch_b"][0]),
            "wo": np.ascontiguousarray(inp["w_out"][0]),
            "w1": np.ascontiguousarray(inp["w_mlp_in"][0]),
            "w2": np.ascontiguousarray(inp["w_mlp_out"][0]),
            "ident": ident,
        })
    return maps


def run_l2(inp, yaT, ybT, S_LEN):
    N = S_LEN // 4
    nc = build_l2(N)
    res = run_bass_kernel_spmd(nc, l2_inputs(inp, yaT, ybT, N, S_LEN), core_ids=list(range(8)))
    out = np.zeros((2, S_LEN, D), np.float32)
    for core in range(8):
        b, j = core // 4, core % 4
        out[b, j * N:(j + 1) * N] = res.results[core]["out"]
    return out


def fused_inputs(inp, S_LEN):
    N = S_LEN // 4
    m1 = l1_inputs(inp, S_LEN)
    dummy = np.zeros((2, 1, 4), ml_dtypes.bfloat16)
    maps = []
    wada = np.ascontiguousarray(inp["w_ada"][0])
    bada = np.ascontiguousarray(inp["b_ada"][0][None])
    ident = np.eye(128, dtype=np.float32)
    wg = np.ascontiguousarray(inp["w_in"][0][:, 6160:6160 + 2 * D])
    g12 = np.ascontiguousarray(np.concatenate([inp["norm1_g"][0], inp["norm2_g"][0]])[None])
    for core in range(8):
        b, j = core // 4, core % 4
        m = dict(m1[core])
        m["wada"] = wada
        m["bada"] = bada
        m["l2_x"] = np.ascontiguousarray(inp["x"][b, :S_LEN][j * N:(j + 1) * N])
        m["g12"] = g12
        m["wg"] = wg
        m["wpa"] = np.ascontiguousarray(inp["w_branch_a"][0])
        m["wpb"] = np.ascontiguousarray(inp["w_branch_b"][0])
        m["wo"] = np.ascontiguousarray(inp["w_out"][0])
        m["w1"] = np.ascontiguousarray(inp["w_mlp_in"][0])
        m["w2"] = np.ascontiguousarray(inp["w_mlp_out"][0])
        m["ident"] = ident
        maps.append(m)
    return maps


def run_fused(inp, S_LEN):
    N = S_LEN // 4
    nc = build_fused(S_LEN)
    res = run_bass_kernel_spmd(nc, fused_inputs(inp, S_LEN), core_ids=list(range(8)))
    out = np.zeros((2, S_LEN, D), np.float32)
    for core in range(8):
        b, j = core // 4, core % 4
        out[b, j * N:(j + 1) * N] = res.results[core]["out"]
    return out


def kernel(**inputs):
    inp = {k: np.asarray(v) for k, v in inputs.items()}
    S_LEN = inp["x"].shape[1]
    return run_fused(inp, S_LEN)
```

```python
import math
from contextlib import ExitStack
import numpy as np
import ml_dtypes
import concourse.bass as bass
import concourse.mybir as mybir
from concourse.bass_utils import run_bass_kernel_spmd

F32 = mybir.dt.float32
BF16 = mybir.dt.bfloat16
I32 = mybir.dt.int32
AF = mybir.ActivationFunctionType
ALU = mybir.AluOpType
AX = mybir.AxisListType

D = 2048
NCH = 16
EPS = 1e-6
TWO_PI = 2.0 * math.pi
MAGIC = 12582912.0
CW1 = 6.28125
CW2 = TWO_PI - 6.28125


class Buf:
    __slots__ = ("name", "w", "r", "excl")

    def __init__(self, name="b", excl=False):
        self.name = name
        self.w = None
        self.r = []
        self.excl = excl


class Sched:
    def __init__(self, nc):
        self.nc = nc
        self.E = {"pe": nc.tensor, "act": nc.scalar, "dve": nc.vector, "pool": nc.gpsimd, "sp": nc.sync}
        self.sem = {k: nc.alloc_semaphore("s_" + k) for k in ("pe", "act", "dve", "pool")}
        self.cnt = {k: 0 for k in self.sem}
        self.waited = {k: {} for k in self.E}
        self.dma_sems = []
        self.rr = 0

    def _deps(self, eng, reads, writes):
        best = {}
        for b in reads:
            if b.w is not None:
                s, v = b.w
                if best.get(s, 0) < v:
                    best[s] = v
        for b in writes:
            if b.w is not None:
                s, v = b.w
                if best.get(s, 0) < v:
                    best[s] = v
            for (s, v) in b.r:
                if best.get(s, 0) < v:
                    best[s] = v
        wd = self.waited[eng]
        e = self.E[eng]
        for s, v in best.items():
            if wd.get(s, 0) < v:
                e.wait_ge(s, v)
                wd[s] = v

    def op(self, eng, fn, reads=(), writes=(), signal=True):
        if any(b.excl for b in reads):
            writes = list(writes) + [b for b in reads if b.excl]
            reads = [b for b in reads if not b.excl]
        self._deps(eng, reads, writes)
        ins = fn()
        if signal:
            self.cnt[eng] += 1
            ins.then_inc(self.sem[eng], 1)
            tok = (self.sem[eng], self.cnt[eng])
            for b in reads:
                b.r.append(tok)
                if len(b.r) > 12:
                    b.r = self._compact(b.r)
            for b in writes:
                b.w = tok
                b.r = []
        return ins

    @staticmethod
    def _compact(lst):
        best = {}
        for s, v in lst:
            if best.get(s, 0) < v:
                best[s] = v
        return list(best.items())

    def new_dma_sem(self, name):
        self.nsem = getattr(self, "nsem", 0) + 1
        d = [self.nc.alloc_semaphore("%s_u%d" % (name, self.nsem)), 0]
        self.dma_sems.append(d)
        return d

    def dma(self, queue, dsem, out, in_, reads=(), writes=(), **kw):
        self._deps(queue, reads, writes)
        ins = self.E[queue].dma_start(out=out, in_=in_, **kw)
        dsem[1] += 16
        ins.then_inc(dsem[0], 16)
        tok = (dsem[0], dsem[1])
        for b in reads:
            b.r.append(tok)
            if len(b.r) > 12:
                b.r = self._compact(b.r)
        for b in writes:
            b.w = tok
            b.r = []
        return ins

    def barrier(self):
        for k, e in self.E.items():
            wd = self.waited[k]
            for s, v in self.dma_sems:
                if v > 0 and wd.get(s, 0) < v:
                    e.wait_ge(s, v)
                    wd[s] = v
            for kk, s in self.sem.items():
                v = self.cnt[kk]
                if v > 0 and wd.get(s, 0) < v:
                    e.wait_ge(s, v)
                    wd[s] = v

    def finish(self):
        self.barrier()

    def rotate(self):
        self.barrier()
        for k in list(self.sem):
            self.sem[k] = self.nc.alloc_semaphore("s2_%s_%d" % (k, self.rr))
            self.cnt[k] = 0
        self.rr += 1


class Ctx:
    def __init__(self, nc):
        self.nc = nc
        self.S = Sched(nc)
        self.n = 0

    def alloc(self, stack, shape, dt, name=None):
        self.n += 1
        nm = (name or "t") + "_%d" % self.n
        t = stack.enter_context(self.nc.sbuf_tensor(nm, list(shape), dt))
        return t.ap(), Buf(nm)


def rsqrt_inplace(cx, t_ap, t_b, src_ap, src_b, mult, eng_mid="act"):
    nc, S = cx.nc, cx.S
    S.op("dve", lambda: nc.vector.tensor_scalar(out=t_ap, in0=src_ap, scalar1=mult, scalar2=EPS, op0=ALU.mult, op1=ALU.add), [src_b], [t_b])
    S.op("act", lambda: nc.scalar.activation(out=t_ap, in_=t_ap, func=AF.Sqrt), [t_b], [t_b])
    S.op("dve", lambda: nc.vector.reciprocal(out=t_ap, in_=t_ap), [t_b], [t_b])


def compute_mod_cols(cx, stack, ps, psb, c_d, wada_d, bada_d, ncols, ones_row, b_ones_row, one11, b_one11):
    nc, S = cx.nc, cx.S
    ccol, b_ccol = cx.alloc(stack, [128, NCH], F32, "ccol")
    dsm = S.new_dma_sem("d_mod")
    S.dma("sp", dsm, ccol, c_d, writes=[b_ccol])
    S.op("act", lambda: nc.scalar.activation(out=ccol, in_=ccol, func=AF.Silu), [b_ccol], [b_ccol])
    row, b_row = cx.alloc(stack, [1, ncols], F32, "modrow")
    brow, b_brow = cx.alloc(stack, [1, ncols], F32, "badarow")
    S.dma("sp", S.new_dma_sem("d_mod2"), brow, bada_d[:, 0:ncols], writes=[b_brow])
    wv = wada_d.rearrange("(kc p) n -> p kc n", p=128)
    with ExitStack() as st2:
        wt = [cx.alloc(st2, [128, NCH, 512], F32, "wadat") for _ in range(2)]
        dws = [S.new_dma_sem("d_wada%d" % i) for i in range(2)]
        for j in range(ncols // 512):
            w_ap, w_b = wt[j % 2]
            q = "sp" if j % 2 == 0 else "pool"
            S.dma(q, dws[j % 2], w_ap, wv[:, :, j * 512:(j + 1) * 512], writes=[w_b])
            for kc in range(NCH):
                S.op("pe", lambda: nc.tensor.matmul(ps[0][0:1, :], lhsT=ccol[:, kc:kc + 1], rhs=w_ap[:, kc, :], start=(kc == 0), stop=(kc == NCH - 1)),
                     [b_ccol, w_b], [psb[0]], signal=(kc == NCH - 1))
            S.op("dve", lambda: nc.vector.tensor_tensor(out=row[:, j * 512:(j + 1) * 512], in0=ps[0][0:1, :], in1=brow[:, j * 512:(j + 1) * 512], op=ALU.add),
                 [psb[0], b_brow], [b_row])
        S.barrier()
    return row, b_row


def row_to_cols(cx, ps_ap, ps_b, row_ap, row_b, n, one11, b_one11, dst_ap, dst_b):
    nc, S = cx.nc, cx.S
    for j in range(n):
        S.op("pe", lambda: nc.tensor.matmul(ps_ap[:, j:j + 1], lhsT=row_ap[0:1, j * 128:(j + 1) * 128], rhs=one11, start=True, stop=True),
             [row_b, b_one11], [ps_b], signal=(j == n - 1))
    S.op("dve", lambda: nc.vector.tensor_copy(out=dst_ap, in_=ps_ap[:, 0:n]), [ps_b], [dst_b])


def norm_transpose_tile(cx, x_src, xt, b_xt, dsem, junk, b_junk, ss, b_ss, xn, b_xn, ident, b_ident, ps, psb,
                        scol, b_scol, shcol, b_shcol, uT, b_uT, col0, queue="sp"):
    nc, S = cx.nc, cx.S
    S.dma(queue, dsem, xt, x_src, writes=[b_xt])
    S.op("act", lambda: nc.scalar.activation(out=junk, in_=xt, func=AF.Square, accum_out=ss), [b_xt], [b_junk, b_ss])
    rsqrt_inplace(cx, ss, b_ss, ss, b_ss, 1.0 / D)
    S.op("act", lambda: nc.scalar.activation(out=xn, in_=xt, func=AF.Identity, scale=ss), [b_xt, b_ss], [b_xn])
    for half in range(2):
        pT = ps[half].bitcast(BF16)
        for k in range(8):
            c = half * 8 + k
            S.op("pe", lambda: nc.tensor.transpose(out=pT[:, k * 128:(k + 1) * 128], in_=xn[:, c * 128:(c + 1) * 128], identity=ident),
                 [b_xn, b_ident], [psb[half]], signal=(k == 7))
        for k in range(8):
            c = half * 8 + k
            if k % 2 == 0:
                S.op("act", lambda: nc.scalar.activation(out=uT[:, c, col0:col0 + 128], in_=pT[:, k * 128:(k + 1) * 128], func=AF.Identity,
                                                         scale=scol[:, c:c + 1], bias=shcol[:, c:c + 1]),
                     [psb[half], b_scol, b_shcol], [b_uT])
            else:
                S.op("dve", lambda: nc.vector.tensor_scalar(out=uT[:, c, col0:col0 + 128], in0=pT[:, k * 128:(k + 1) * 128],
                                                            scalar1=scol[:, c:c + 1], scalar2=shcol[:, c:c + 1], op0=ALU.mult, op1=ALU.add),
                     [psb[half], b_scol, b_shcol], [b_uT])


NFM = 1168
NTM = 640


CW = 256


def build_l1(S_LEN, env=None):
    T = S_LEN // 128
    G = S_LEN // 512
    if env is None:
        nc = bass.Bass("TRN2", target_bir_lowering=False)
        x_d = nc.dram_tensor("x", [S_LEN, D], F32, kind="ExternalInput").ap()
        c_d = nc.dram_tensor("c", [128, NCH], F32, kind="ExternalInput").ap()
        wada_d = nc.dram_tensor("wada", [D, 2 * D], F32, kind="ExternalInput").ap()
        bada_d = nc.dram_tensor("bada", [1, 2 * D], F32, kind="ExternalInput").ap()
    else:
        nc = env["nc"]
        x_d, c_d, wada_d, bada_d = env["x"], env["c"], env["wada"], env["bada"]
    pos_d = nc.dram_tensor("pos", [1, S_LEN], I32, kind="ExternalInput").ap()
    g1_d = nc.dram_tensor("g1", [1, D], F32, kind="ExternalInput").ap()
    wfm_d = nc.dram_tensor("wfm", [D, NFM], F32, kind="ExternalInput").ap()
    wtm_d = nc.dram_tensor("wtm", [D, NTM], F32, kind="ExternalInput").ap()
    consts_d = nc.dram_tensor("consts", [128, 5, 128], F32, kind="ExternalInput").ap()
    fcol_d = nc.dram_tensor("fcol", [128, 1], F32, kind="ExternalInput").ap()
    qkg_d = nc.dram_tensor("qkg", [128, 2], F32, kind="ExternalInput").ap()
    lam_d = nc.dram_tensor("lam", [1, 4, 64], F32, kind="ExternalInput").ap()
    subg_d = nc.dram_tensor("subg", [128, 1], F32, kind="ExternalInput").ap()
    aup_d = nc.dram_tensor("aup", [32, 128], F32, kind="ExternalInput").ap()
    gout_d = nc.dram_tensor("gout", [128, 2], F32, kind="ExternalInput").ap()
    if env is None:
        yaT_d = nc.dram_tensor("yaT", [256, S_LEN], BF16, kind="ExternalOutput").ap()
        ybT_d = nc.dram_tensor("ybT", [256, S_LEN], BF16, kind="ExternalOutput").ap()
        ycat_fcw = None
    else:
        ycat_fcw = env["ycat"].rearrange("c f w -> f c w")
    NPG = 512 // CW
    qT_s = nc.dram_tensor("qT_s", [2, 128, S_LEN], BF16, kind="Internal").ap()
    kT_s = nc.dram_tensor("kT_s", [2, 128, S_LEN], BF16, kind="Internal").ap()
    v_s = nc.dram_tensor("v_s", [2, 128, T, 128], BF16, kind="Internal").ap()

    if env is None:
        cx = Ctx(nc)
        ps = [nc.alloc_psum_tensor("ps%d" % i, [128, 512], F32).ap() for i in range(8)]
        psb = [Buf("ps%d" % i, excl=True) for i in range(8)]
    else:
        cx, ps, psb = env["cx"], env["ps"], env["psb"]
    S = cx.S

    with ExitStack() as gst:
        cst, b_cst = cx.alloc(gst, [128, 5, 128], F32, "cst")
        d_c = S.new_dma_sem("d_c")
        S.dma("sp", S.new_dma_sem("d_cx0"), cst, consts_d, writes=[b_cst])
        identf, Ubd, Lbd, Rm, bones_f = [cst[:, i, :] for i in range(5)]
        ident, b_ident = cx.alloc(gst, [128, 128], BF16, "ident")
        ones_bf, b_ones = cx.alloc(gst, [128, 128], BF16, "ones")
        bones, b_bones = cx.alloc(gst, [128, 128], BF16, "bones")
        ones_f, b_ones_f = cx.alloc(gst, [128, 128], F32, "ones_f")
        S.op("dve", lambda: nc.vector.tensor_copy(out=ident, in_=identf), [b_cst], [b_ident])
        S.op("dve", lambda: nc.vector.tensor_copy(out=bones, in_=bones_f), [b_cst], [b_bones])
        S.op("pool", lambda: nc.gpsimd.memset(ones_bf, 1.0), [], [b_ones])
        S.op("pool", lambda: nc.gpsimd.memset(ones_f, 1.0), [], [b_ones_f])
        one11 = ones_f[0:1, 0:1]
        fcol, b_fcol = cx.alloc(gst, [128, 1], F32, "fcol")
        qkg, b_qkg = cx.alloc(gst, [128, 2], F32, "qkg")
        subg, b_subg = cx.alloc(gst, [128, 1], F32, "subg")
        gout, b_gout = cx.alloc(gst, [128, 2], F32, "gout")
        S.dma("sp", S.new_dma_sem("d_cx1"), fcol, fcol_d, writes=[b_fcol])
        S.dma("sp", S.new_dma_sem("d_cx2"), qkg, qkg_d, writes=[b_qkg])
        S.dma("sp", S.new_dma_sem("d_cx3"), subg, subg_d, writes=[b_subg])
        S.dma("sp", S.new_dma_sem("d_cx4"), gout, gout_d, writes=[b_gout])
        S.op("dve", lambda: nc.vector.tensor_scalar(out=qkg[:, 0:1], in0=qkg[:, 0:1], scalar1=0.125, scalar2=None, op0=ALU.mult), [b_qkg], [b_qkg])
        S.op("dve", lambda: nc.vector.tensor_scalar(out=subg, in0=subg, scalar1=0.8, scalar2=None, op0=ALU.mult), [b_subg], [b_subg])
        lam4, b_lam4 = cx.alloc(gst, [1, 4, 64], F32, "lam4")
        S.dma("sp", S.new_dma_sem("d_cx5"), lam4, lam_d, writes=[b_lam4])
        lp, b_lp = cx.alloc(gst, [1, 2, 64], F32, "lp")
        ls, b_ls = cx.alloc(gst, [1, 2], F32, "ls")
        S.op("dve", lambda: nc.vector.tensor_tensor(out=lp[:, 0, :], in0=lam4[:, 0, :], in1=lam4[:, 1, :], op=ALU.mult), [b_lam4], [b_lp])
        S.op("dve", lambda: nc.vector.tensor_tensor(out=lp[:, 1, :], in0=lam4[:, 2, :], in1=lam4[:, 3, :], op=ALU.mult), [b_lam4], [b_lp])
        S.op("dve", lambda: nc.vector.reduce_sum(out=ls, in_=lp, axis=AX.X), [b_lp], [b_ls])
        S.op("act", lambda: nc.scalar.activation(out=ls, in_=ls, func=AF.Exp), [b_ls], [b_ls])
        nl1, b_nl1 = cx.alloc(gst, [1, 1], F32, "nl1")
        S.op("dve", lambda: nc.vector.tensor_tensor(out=nl1, in0=ls[:, 1:2], in1=ls[:, 0:1], op=ALU.subtract), [b_ls], [b_nl1])
        S.op("dve", lambda: nc.vector.tensor_scalar(out=nl1, in0=nl1, scalar1=-0.2, scalar2=None, op0=ALU.add), [b_nl1], [b_nl1])
        negl, b_negl = cx.alloc(gst, [128, 1], F32, "negl")
        S.op("pe", lambda: nc.tensor.matmul(ps[7][:, 0:1], lhsT=ones_f[0:1, :], rhs=nl1, start=True, stop=True), [b_ones_f, b_nl1], [psb[7]])
        S.op("dve", lambda: nc.vector.tensor_copy(out=negl, in_=ps[7][:, 0:1]), [psb[7]], [b_negl])
        scol, b_scol = cx.alloc(gst, [128, NCH], F32, "scol")
        shcol, b_shcol = cx.alloc(gst, [128, NCH], F32, "shcol")
        with ExitStack() as st:
            row, b_row = compute_mod_cols(cx, st, ps, psb, c_d, wada_d, bada_d, 2 * D, None, None, one11, b_ones_f)
            g1r, b_g1r = cx.alloc(st, [1, D], F32, "g1r")
            S.dma("sp", S.new_dma_sem("d_cx6"), g1r, g1_d, writes=[b_g1r])
            S.op("dve", lambda: nc.vector.scalar_tensor_tensor(out=row[:, D:2 * D], in0=row[:, D:2 * D], scalar=1.0, in1=g1r, op0=ALU.add, op1=ALU.mult),
                 [b_row, b_g1r], [b_row])
            row_to_cols(cx, ps[1], psb[1], row[:, 0:D], b_row, NCH, one11, b_ones_f, shcol, b_shcol)
            row_to_cols(cx, ps[2], psb[2], row[:, D:2 * D], b_row, NCH, one11, b_ones_f, scol, b_scol)
            S.barrier()
        import os
        STOP = int(os.environ.get("L1_STOP", "9"))
        if STOP == 0:
            S.finish()
            return nc
        aupf, b_aupf = cx.alloc(gst, [32, 128], F32, "aupf")
        aup, b_aup = cx.alloc(gst, [32, 128], BF16, "aup")
        S.dma("sp", S.new_dma_sem("d_cx7"), aupf, aup_d, writes=[b_aupf])
        S.op("dve", lambda: nc.vector.tensor_copy(out=aup, in_=aupf), [b_aupf], [b_aup])

        with ExitStack() as st:
            wfm, b_wfm = cx.alloc(st, [128, NCH, NFM], BF16, "wfm")
            wtm, b_wtm = cx.alloc(st, [128, NCH, NTM], BF16, "wtm")
            with ExitStack() as st2:
                stg = [cx.alloc(st2, [128, NFM], F32, "wstg") for _ in range(2)]
                dws = [S.new_dma_sem("d_ws%d" % i) for i in range(2)]
                i = 0
                for (src, dst, b_dst, ncol) in ((wfm_d, wfm, b_wfm, NFM), (wtm_d, wtm, b_wtm, NTM)):
                    for kc in range(NCH):
                        sa, sb = stg[i % 2]
                        S.dma("sp" if i % 2 == 0 else "pool", dws[i % 2], sa[:, 0:ncol], src[kc * 128:(kc + 1) * 128, :], writes=[sb])
                        if i % 2 == 0:
                            S.op("dve", lambda: nc.vector.tensor_copy(out=dst[:, kc, :], in_=sa[:, 0:ncol]), [sb], [b_dst])
                        else:
                            S.op("act", lambda: nc.scalar.copy(out=dst[:, kc, :], in_=sa[:, 0:ncol]), [sb], [b_dst])
                        i += 1
                S.barrier()
            if STOP == 1:
                S.finish()
                return nc
            NXB = 3
            xts = [cx.alloc(st, [128, D], F32, "xt") for _ in range(NXB)]
            dxs = [S.new_dma_sem("d_x%d" % i) for i in range(NXB)]
            junk, b_junk = cx.alloc(st, [128, D], BF16, "junk")
            sss = [cx.alloc(st, [128, 1], F32, "ss") for _ in range(2)]
            xns = [cx.alloc(st, [128, D], BF16, "xn") for _ in range(2)]
            uTs = [cx.alloc(st, [128, NCH, 512], BF16, "uT") for _ in range(2)]
            posi, b_posi = cx.alloc(st, [128, 512], I32, "posi")
            posf, b_posf = cx.alloc(st, [128, 512], F32, "posf")
            ang, b_ang = cx.alloc(st, [128, 512], F32, "ang")
            sinT, b_sinT = cx.alloc(st, [128, 512], F32, "sinT")
            cosT, b_cosT = cx.alloc(st, [128, 512], F32, "cosT")
            d_pos = S.new_dma_sem("d_pos")
            rk, b_rk = cx.alloc(st, [128, 512], F32, "rk")
            rr, b_rr = cx.alloc(st, [128, 512], F32, "rr")
            qf, b_qf = cx.alloc(st, [128, 512], F32, "qf")
            sq, b_sq = cx.alloc(st, [128, 512], BF16, "sq")
            rt, b_rt = cx.alloc(st, [128, 512], F32, "rt")
            qn, b_qn = cx.alloc(st, [128, 512], F32, "qn")
            qa, b_qa = cx.alloc(st, [128, 512], F32, "qa")
            qb, b_qb = cx.alloc(st, [128, 512], F32, "qb")
            qos = [cx.alloc(st, [128, 512], BF16, "qo") for _ in range(2)]
            d_qo = [S.new_dma_sem("d_qo%d" % i) for i in range(2)]
            gqT, b_gqT = cx.alloc(st, [128, 512], BF16, "gqT")
            gkT, b_gkT = cx.alloc(st, [128, 512], BF16, "gkT")
            sr, b_sr = cx.alloc(st, [128, 2, 512], F32, "sr")
            alow, b_alow = cx.alloc(st, [32, 512], BF16, "alow")
            S.op("pool", lambda: nc.gpsimd.memset(alow, 1.0), [], [b_alow])
            vts = [cx.alloc(st, [128, 256], BF16, "vt") for _ in range(2)]
            d_vt = [S.new_dma_sem("d_vt%d" % i) for i in range(2)]
            gv, b_gv = cx.alloc(st, [128, 256], BF16, "gv")
            e1, b_e1 = cx.alloc(st, [128, 128], F32, "e1")
            lgl, b_lgl = cx.alloc(st, [128, 128], F32, "lgl")
            ebT, b_ebT = cx.alloc(st, [128, 128], F32, "ebT")
            enbT, b_enbT = cx.alloc(st, [128, 128], F32, "enbT")
            erb, b_erb = cx.alloc(st, [128, 128], F32, "erb")
            qtl, b_qtl = cx.alloc(st, [128, 128], BF16, "qtl")
            ktl, b_ktl = cx.alloc(st, [128, 128], BF16, "ktl")
            khat, b_khat = cx.alloc(st, [128, 128], BF16, "khat")
            gksb, b_gksb = cx.alloc(st, [128, 128], F32, "gksb")
            atsb, b_atsb = cx.alloc(st, [128, 128], F32, "atsb")
            atm, b_atm = cx.alloc(st, [128, 128], BF16, "atm")
            state, b_state = cx.alloc(st, [128, 256], F32, "state")
            state_bf, b_state_bf = cx.alloc(st, [128, 256], BF16, "state_bf")
            S.op("pool", lambda: nc.gpsimd.memset(state, 0.0), [], [b_state])
            S.op("pool", lambda: nc.gpsimd.memset(state_bf, 0.0), [], [b_state_bf])
            oI, b_oI = cx.alloc(st, [128, 2, 128], F32, "oI")
            osb, b_osb = cx.alloc(st, [128, 2, 128], F32, "osb")
            osq, b_osq = cx.alloc(st, [128, 2, 128], BF16, "osq")
            ors, b_ors = cx.alloc(st, [128, 128], F32, "ors")
            otmp, b_otmp = cx.alloc(st, [128, 128], F32, "otmp")
            ybs = [cx.alloc(st, [128, 2, 512], BF16, "yb") for _ in range(2)]
            d_yb = [S.new_dma_sem("d_yb%d" % i) for i in range(2)]
            p_fm, pb_fm = ps[2], psb[2]
            p_nr, pb_nr = ps[3], psb[3]
            p_tmA, pb_tmA = ps[4], psb[4]
            p5 = [ps[5][:, i * 128:(i + 1) * 128] for i in range(4)]
            pb5 = [psb[5]] * 4
            p_gk, p_z, p_cb, p_crb = p5
            import os as _os
            pb_gk, pb_z, pb_cb, pb_crb = pb5
            p_at, pb_at = ps[6][:, 0:128], psb[6]
            p_up, pb_up = ps[6][:, 128:384], psb[6]
            p_ss, pb_ss = ps[6][:, 384:512], psb[6]
            p_oT = [ps[7][:, i * 128:(i + 1) * 128] for i in range(2)]
            pb_oT = [psb[7]] * 2
            p_oI = [ps[7][:, (2 + i) * 128:(3 + i) * 128] for i in range(2)]
            pb_oI = [psb[7]] * 2

            xi = 0
            for g in range(G):
                uT, b_uT = uTs[g % 2]
                for tt in range(4):
                    t = g * 4 + tt
                    xt, b_xt = xts[xi % NXB]
                    ss, b_ss = sss[xi % 2]
                    xn, b_xn = xns[xi % 2]
                    norm_transpose_tile(cx, x_d[t * 128:(t + 1) * 128, :], xt, b_xt, dxs[xi % NXB], junk, b_junk, ss, b_ss, xn, b_xn,
                                        ident, b_ident, ps, psb, scol, b_scol, shcol, b_shcol, uT, b_uT, tt * 128,
                                        queue=("sp" if xi % 2 == 0 else "pool"))
                    xi += 1
                if STOP == 3:
                    S.finish()
                    return nc
                S.dma("sp", d_pos, posi, pos_d[0:1, g * 512:(g + 1) * 512].partition_broadcast(128), writes=[b_posi])
                S.op("dve", lambda: nc.vector.tensor_copy(out=posf, in_=posi), [b_posi], [b_posf])
                S.op("dve", lambda: nc.vector.tensor_scalar(out=ang, in0=posf, scalar1=fcol[:, 0:1], scalar2=None, op0=ALU.mult), [b_posf, b_fcol], [b_ang])
                for (dst, b_dst, off) in ((sinT, b_sinT, 0.0), (cosT, b_cosT, math.pi / 2)):
                    S.op("dve", lambda: nc.vector.tensor_scalar(out=rk, in0=ang, scalar1=1.0 / TWO_PI, scalar2=off / TWO_PI, op0=ALU.mult, op1=ALU.add),
                         [b_ang], [b_rk])
                    S.op("dve", lambda: nc.vector.tensor_scalar(out=rk, in0=rk, scalar1=MAGIC, scalar2=None, op0=ALU.add), [b_rk], [b_rk])
                    S.op("dve", lambda: nc.vector.tensor_scalar(out=rk, in0=rk, scalar1=-MAGIC, scalar2=None, op0=ALU.add), [b_rk], [b_rk])
                    S.op("dve", lambda: nc.vector.scalar_tensor_tensor(out=rr, in0=rk, scalar=-CW1, in1=ang, op0=ALU.mult, op1=ALU.add), [b_rk, b_ang], [b_rr])
                    S.op("dve", lambda: nc.vector.scalar_tensor_tensor(out=rr, in0=rk, scalar=-CW2, in1=rr, op0=ALU.mult, op1=ALU.add), [b_rk, b_rr], [b_rr])
                    S.op("act", lambda: nc.scalar.activation(out=dst, in_=rr, func=AF.Sin, bias=off), [b_rr], [b_dst])
                if STOP == 4:
                    S.finish()
                    return nc
                for j in range(9):
                    m = 16 if j == 8 else 128
                    po = p_fm[0:m, :]
                    for kc in range(NCH):
                        S.op("pe", lambda: nc.tensor.matmul(po, lhsT=wfm[:, kc, j * 128:j * 128 + m], rhs=uT[:, kc, :], start=(kc == 0), stop=(kc == NCH - 1)),
                             [b_wfm, b_uT], [pb_fm], signal=(kc == NCH - 1))
                    if j < 4:
                        isq = j < 2
                        h = j % 2
                        gcol = qkg[:, 0:1] if isq else qkg[:, 1:2]
                        S.op("act", lambda: nc.scalar.copy(out=qf, in_=po), [pb_fm], [b_qf])
                        S.op("act", lambda: nc.scalar.activation(out=sq, in_=po, func=AF.Square), [pb_fm], [b_sq])
                        S.op("pe", lambda: nc.tensor.matmul(p_nr, lhsT=bones, rhs=sq, start=True, stop=True), [b_bones, b_sq], [pb_nr])
                        rsqrt_inplace(cx, rt, b_rt, p_nr, pb_nr, 1.0 / 64)
                        S.op("dve", lambda: nc.vector.scalar_tensor_tensor(out=qn, in0=qf, scalar=gcol, in1=rt, op0=ALU.mult, op1=ALU.mult),
                             [b_qf, b_qkg, b_rt], [b_qn])
                        S.op("pe", lambda: nc.tensor.matmul(p_nr, lhsT=Rm, rhs=qn, start=True, stop=True), [b_cst, b_qn], [pb_nr])
                        S.op("pool", lambda: nc.gpsimd.tensor_tensor(out=qa, in0=qn, in1=cosT, op=ALU.mult), [b_qn, b_cosT], [b_qa])
                        S.op("dve", lambda: nc.vector.tensor_tensor(out=qb, in0=p_nr, in1=sinT, op=ALU.mult), [pb_nr, b_sinT], [b_qb])
                        qo, b_qo = qos[j % 2]
                        S.op("pool", lambda: nc.gpsimd.tensor_tensor(out=qo, in0=qa, in1=qb, op=ALU.add), [b_qa, b_qb], [b_qo])
                        dst = (qT_s if isq else kT_s)[h, :, g * 512:(g + 1) * 512]
                        S.dma("sp", d_qo[j % 2], dst, qo, reads=[b_qo])
                    elif j == 4:
                        S.op("act", lambda: nc.scalar.activation(out=gqT, in_=po, func=AF.Copy, scale=128.0 ** -0.5), [pb_fm], [b_gqT])
                    elif j == 5:
                        S.op("act", lambda: nc.scalar.copy(out=gkT, in_=po), [pb_fm], [b_gkT])
                    elif j in (6, 7):
                        S.op("act", lambda: nc.scalar.activation(out=sr[:, j - 6, :], in_=po, func=AF.Silu), [pb_fm], [b_sr])
                    else:
                        S.op("act", lambda: nc.scalar.copy(out=alow[0:16, :], in_=po), [pb_fm], [b_alow])
                if STOP == 5:
                    S.finish()
                    return nc
                yb, b_yb = ybs[g % 2]
                for tt in range(4):
                    t = g * 4 + tt
                    cs = slice(tt * 128, (tt + 1) * 128)
                    for kc in range(NCH):
                        S.op("pe", lambda: nc.tensor.matmul(p_tmA, lhsT=uT[:, kc, cs], rhs=wtm[:, kc, 0:512], start=(kc == 0), stop=(kc == NCH - 1)),
                             [b_uT, b_wtm], [pb_tmA], signal=(kc == NCH - 1))
                    for kc in range(NCH):
                        S.op("pe", lambda: nc.tensor.matmul(p_gk, lhsT=uT[:, kc, cs], rhs=wtm[:, kc, 512:640], start=(kc == 0), stop=(kc == NCH - 1)),
                             [b_uT, b_wtm], [pb_gk], signal=(kc == NCH - 1))
                    if STOP == 7:
                        S.finish()
                        return nc
                    vt, b_vt = vts[t % 2]
                    S.op("act", lambda: nc.scalar.copy(out=vt, in_=p_tmA[:, 0:256]), [pb_tmA], [b_vt])
                    for hh in range(2):
                        S.dma("sp", d_vt[t % 2], v_s[hh, :, t, :], vt[:, hh * 128:(hh + 1) * 128], reads=[b_vt])
                    if STOP == 8:
                        S.finish()
                        return nc
                    S.op("act", lambda: nc.scalar.copy(out=gv, in_=p_tmA[:, 256:512]), [pb_tmA], [b_gv])
                    if STOP == 6:
                        S.finish()
                        return nc
                    S.op("pe", lambda: nc.tensor.matmul(p_z, lhsT=alow[0:32, cs], rhs=aup, start=True, stop=True), [b_alow, b_aup], [pb_z])
                    S.op("act", lambda: nc.scalar.activation(out=e1, in_=p_z, func=AF.Exp, scale=-1.0), [pb_z], [b_e1])
                    S.op("act", lambda: nc.scalar.activation(out=lgl, in_=e1, func=AF.Ln, bias=1.0), [b_e1], [b_lgl])
                    if STOP == 11:
                        S.finish()
                        return nc
                    S.op("pe", lambda: nc.tensor.matmul(p_cb, lhsT=lgl, rhs=Ubd, start=True, stop=True), [b_lgl, b_cst], [pb_cb])
                    S.op("pe", lambda: nc.tensor.matmul(p_crb, lhsT=Lbd, rhs=lgl, start=True, stop=True), [b_lgl, b_cst], [pb_crb])
                    S.op("act", lambda: nc.scalar.activation(out=ebT, in_=p_cb, func=AF.Exp, scale=-1.0 / 16), [pb_cb], [b_ebT])
                    S.op("act", lambda: nc.scalar.activation(out=enbT, in_=p_cb, func=AF.Exp, scale=1.0 / 16), [pb_cb], [b_enbT])
                    S.op("act", lambda: nc.scalar.activation(out=erb, in_=p_crb, func=AF.Exp, scale=-1.0 / 16), [pb_crb], [b_erb])
                    if STOP == 12:
                        S.finish()
                        return nc
                    S.op("dve", lambda: nc.vector.tensor_tensor(out=qtl, in0=gqT[:, cs], in1=ebT, op=ALU.mult), [b_gqT, b_ebT], [b_qtl])
                    S.op("pool", lambda: nc.gpsimd.tensor_tensor(out=ktl, in0=gkT[:, cs], in1=enbT, op=ALU.mult), [b_gkT, b_enbT], [b_ktl])
                    S.op("act", lambda: nc.scalar.copy(out=gksb, in_=p_gk), [pb_gk], [b_gksb])
                    S.op("dve", lambda: nc.vector.tensor_tensor(out=khat, in0=gksb, in1=erb, op=ALU.mult), [b_gksb, b_erb], [b_khat])
                    S.op("pe", lambda: nc.tensor.matmul(p_at, lhsT=ktl, rhs=qtl, start=True, stop=True), [b_ktl, b_qtl], [pb_at])
                    S.op("act", lambda: nc.scalar.copy(out=atsb, in_=p_at), [pb_at], [b_atsb])
                    S.op("pool", lambda: nc.gpsimd.tensor_tensor(out=atm, in0=atsb, in1=Ubd, op=ALU.mult), [b_atsb, b_cst], [b_atm])
                    if STOP == 10:
                        S.finish()
                        return nc
                    for vh in range(2):
                        S.op("pe", lambda: nc.tensor.matmul(p_oT[vh], lhsT=gv[:, vh * 128:(vh + 1) * 128], rhs=atm, start=True, stop=True),
                             [b_gv, b_atm], [pb_oT[vh]])
                    for ch in range(2):
                        c0 = ch * 64
                        for vh in range(2):
                            S.op("pe", lambda: nc.tensor.matmul(p_oI[vh][:, c0:c0 + 64], lhsT=state_bf[:, vh * 128:(vh + 1) * 128], rhs=qtl[:, c0:c0 + 64],
                                                                start=True, stop=True), [b_state_bf, b_qtl], [pb_oI[vh]])
                        S.op("pe", lambda: nc.tensor.matmul(p_up, lhsT=khat[c0:c0 + 64, :], rhs=gv[c0:c0 + 64, :], start=True, stop=True),
                             [b_khat, b_gv], [pb_up])
                        S.op("dve", lambda: nc.vector.scalar_tensor_tensor(out=state, in0=state, scalar=ebT[:, c0 + 63:c0 + 64], in1=p_up, op0=ALU.mult, op1=ALU.add),
                             [b_state, b_ebT, pb_up], [b_state])
                        S.op("act", lambda: nc.scalar.copy(out=state_bf, in_=state), [b_state], [b_state_bf])
                    for vh in range(2):
                        S.op("act", lambda: nc.scalar.copy(out=oI[:, vh, :], in_=p_oI[vh]), [pb_oI[vh]], [b_oI])
                        S.op("dve", lambda: nc.vector.tensor_tensor(out=osb[:, vh, :], in0=p_oT[vh], in1=oI[:, vh, :], op=ALU.add), [pb_oT[vh], b_oI], [b_osb])
                    S.op("act", lambda: nc.scalar.activation(out=osq, in_=osb, func=AF.Square), [b_osb], [b_osq])
                    for vh in range(2):
                        S.op("pe", lambda: nc.tensor.matmul(p_ss, lhsT=ones_bf, rhs=osq[:, vh, :], start=(vh == 0), stop=(vh == 1)), [b_ones, b_osq], [pb_ss],
                             signal=(vh == 1))
                    rsqrt_inplace(cx, ors, b_ors, p_ss, pb_ss, 1.0 / 256)
                    for vh in range(2):
                        S.op("dve", lambda: nc.vector.scalar_tensor_tensor(out=otmp, in0=osb[:, vh, :], scalar=gout[:, vh:vh + 1], in1=ors, op0=ALU.mult, op1=ALU.mult),
                             [b_osb, b_gout, b_ors], [b_otmp])
                        S.op("pool", lambda: nc.gpsimd.tensor_tensor(out=yb[:, vh, cs], in0=otmp, in1=sr[:, vh, cs], op=ALU.mult), [b_otmp, b_sr], [b_yb])
                if ycat_fcw is None:
                    S.dma("pool", d_yb[g % 2], ybT_d[:, g * 512:(g + 1) * 512].rearrange("(v p) n -> p v n", p=128), yb, reads=[b_yb])
                else:
                    for v in range(2):
                        S.dma("pool", d_yb[g % 2], ycat_fcw[256 + v * 128:256 + (v + 1) * 128, g * NPG:(g + 1) * NPG, :],
                              yb[:, v, :].rearrange("p (c w) -> p c w", w=CW), reads=[b_yb])
            S.barrier()

        if STOP == 2:
            S.finish()
            return nc
        S.rotate()
        with ExitStack() as st:
            qT, b_qT = cx.alloc(st, [128, S_LEN], BF16, "qT")
            qT1, b_qT1 = cx.alloc(st, [128, S_LEN], BF16, "qT1")
            S.op("pool", lambda: nc.gpsimd.memset(qT, 0.0), [], [b_qT])
            S.op("pool", lambda: nc.gpsimd.memset(qT1, 0.0), [], [b_qT1])
            kT, b_kT = cx.alloc(st, [128, S_LEN], BF16, "kT")
            vv, b_vv = cx.alloc(st, [128, T, 128], BF16, "vv")
            d_ld4 = [S.new_dma_sem("d_ld%d" % i) for i in range(4)]
            pts = [[cx.alloc(st, [128, 512], BF16, "pt") for _ in range(2)] for _ in range(2)]
            rl, b_rl = cx.alloc(st, [128, 2, 512], F32, "rl")
            t01, b_t01 = cx.alloc(st, [128, 2, 512], F32, "t01")
            of, b_of = cx.alloc(st, [128, 512], F32, "of")
            osq2, b_osq2 = cx.alloc(st, [128, 512], BF16, "osq2")
            rs2, b_rs2 = cx.alloc(st, [128, 512], F32, "rs2")
            yas = [cx.alloc(st, [128, 512], BF16, "ya") for _ in range(2)]
            d_ya = [S.new_dma_sem("d_ya%d" % i) for i in range(2)]
            pS = [[ps[0], ps[1]], [ps[2], ps[3]]]
            pSb = [[psb[0], psb[1]], [psb[2], psb[3]]]
            pO, pOb = [ps[4], ps[5]], [psb[4], psb[5]]
            pL, pLb = [ps[6], ps[7]], [psb[6], psb[7]]
            it = 0
            for h in range(2):
                nsp = max(1, S_LEN // 4096)
                for i in range(nsp):
                    sl = slice(i * (S_LEN // nsp), (i + 1) * (S_LEN // nsp))
                    S.dma("sp", d_ld4[0], qT[0:64, sl], qT_s[h, 0:64, sl], writes=[b_qT])
                    S.dma("sp", d_ld4[1], qT1[64:128, sl], qT_s[h, 64:128, sl], writes=[b_qT1])
                    S.dma("pool", d_ld4[2], kT[:, sl], kT_s[h, :, sl], writes=[b_kT])
                    tl = slice(i * (T // nsp), (i + 1) * (T // nsp))
                    S.dma("sp", d_ld4[3], vv[:, tl, :], v_s[h, :, tl, :], writes=[b_vv])
                for g in range(G):
                    nk = 4 * g + 4
                    qs = slice(g * 512, (g + 1) * 512)
                    def emit_S(kt, itn):
                        ks = slice(kt * 128, (kt + 1) * 128)
                        for c in range(2):
                            qsel = qT if c == 0 else qT1
                            S.op("pe", lambda: nc.tensor.matmul(pS[itn % 2][c], lhsT=kT[:, ks], rhs=qsel[:, qs], start=True, stop=True),
                                 [b_kT, b_qT, b_qT1], [pSb[itn % 2][c]])

                    emit_S(0, it)
                    for kt in range(nk):
                        if kt + 1 < nk:
                            emit_S(kt + 1, it + 1)
                        for c in range(2):
                            pSc, pScb = pS[it % 2][c], pSb[it % 2][c]
                            pt, b_pt = pts[it % 2][c]
                            S.op("act", lambda: nc.scalar.activation(out=pt, in_=pSc, func=AF.Exp), [pScb], [b_pt])
                            if kt >= 4 * g:
                                S.op("pool", lambda: nc.gpsimd.affine_select(out=pt, in_=pt, pattern=[[1, 512]], compare_op=ALU.is_ge, fill=0.0,
                                                                            base=g * 512 - kt * 128, channel_multiplier=-1), [b_pt], [b_pt])
                        for c in range(2):
                            pt, b_pt = pts[it % 2][c]
                            S.op("pe", lambda: nc.tensor.matmul(pO[c], lhsT=vv[:, kt, :], rhs=pt, start=(kt == 0), stop=(kt == nk - 1)),
                                 [b_vv, b_pt], [pOb[c], pLb[c]], signal=False)
                            S.op("pe", lambda: nc.tensor.matmul(pL[c], lhsT=ones_bf, rhs=pt, start=(kt == 0), stop=(kt == nk - 1)),
                                 [b_vv, b_pt, b_ones], [pOb[c], pLb[c]])
                        it += 1
                    for c in range(2):
                        S.op("dve", lambda: nc.vector.reciprocal(out=rl[:, c, :], in_=pL[c]), [pLb[c]], [b_rl])
                        S.op("dve", lambda: nc.vector.tensor_tensor(out=t01[:, c, :], in0=pO[c], in1=rl[:, c, :], op=ALU.mult), [pOb[c], b_rl], [b_t01])
                    S.op("dve", lambda: nc.vector.scalar_tensor_tensor(out=of, in0=t01[:, 1, :], scalar=negl[:, 0:1], in1=t01[:, 0, :], op0=ALU.mult, op1=ALU.add),
                         [b_t01, b_negl], [b_of])
                    S.op("act", lambda: nc.scalar.activation(out=osq2, in_=of, func=AF.Square), [b_of], [b_osq2])
                    pE, pEb = pS[it % 2][0], pSb[it % 2][0]
                    S.op("pe", lambda: nc.tensor.matmul(pE, lhsT=ones_bf, rhs=osq2, start=True, stop=True), [b_ones, b_osq2], [pEb])
                    rsqrt_inplace(cx, rs2, b_rs2, pE, pEb, 1.0 / 128)
                    ya, b_ya = yas[g % 2]
                    S.op("dve", lambda: nc.vector.scalar_tensor_tensor(out=ya, in0=of, scalar=subg[:, 0:1], in1=rs2, op0=ALU.mult, op1=ALU.mult),
                         [b_of, b_subg, b_rs2], [b_ya])
                    if ycat_fcw is None:
                        S.dma("sp", d_ya[g % 2], yaT_d[h * 128:(h + 1) * 128, qs], ya, reads=[b_ya])
                    else:
                        S.dma("sp", d_ya[g % 2], ycat_fcw[h * 128:(h + 1) * 128, g * NPG:(g + 1) * NPG, :],
                              ya.rearrange("p (c w) -> p c w", w=CW), reads=[b_ya])
            S.barrier()
        if env is None:
            S.finish()
        else:
            S.rotate()
    return nc


def host_consts():
    ident = np.eye(128, dtype=np.float32)
    U = np.zeros((128, 128), np.float32)
    L = np.zeros((128, 128), np.float32)
    for a in range(128):
        for b in range(128):
            if a // 64 == b // 64:
                if a <= b:
                    U[a, b] = 1.0
                if a > b:
                    L[a, b] = 1.0
    R = np.zeros((128, 128), np.float32)
    for cc in range(2):
        for d in range(8):
            R[64 * cc + d + 8, 64 * cc + d] = -1.0
            R[64 * cc + d, 64 * cc + d + 8] = 1.0
    bones = np.zeros((128, 128), np.float32)
    bones[0:64, 0:64] = 1.0
    bones[64:128, 64:128] = 1.0
    consts = np.stack([ident, U, L, R, bones], axis=1).astype(np.float32)
    inv_freq = (500000.0 ** (-(np.arange(0, 16, 2, dtype=np.float32)) / 16.0)).astype(np.float32)
    fcol = np.zeros((128, 1), np.float32)
    for r in range(128):
        d = r % 64
        if d < 16:
            fcol[r, 0] = inv_freq[d % 8]
    return consts, fcol


def l1_inputs(inp, S_LEN):
    consts, fcol = host_consts()
    w_in = inp["w_in"][0]
    o_daq, o_dak, o_dav = 0, 1024, 2048
    o_gq, o_gk, o_gv, o_gr, o_ga = 3072, 3584, 4096, 5120, 6144
    maps = []
    for core in range(8):
        b, hg = core // 4, core % 4
        sl256 = slice(256 * hg, 256 * hg + 256)
        sl128 = slice(128 * hg, 128 * hg + 128)
        gk_w = w_in[:, o_gk:o_gk + 512][:, sl128]
        wfm = np.concatenate([w_in[:, o_daq:o_daq + 1024][:, sl256], w_in[:, o_dak:o_dak + 1024][:, sl256],
                              w_in[:, o_gq:o_gq + 512][:, sl128], gk_w, w_in[:, o_gr:o_gr + 1024][:, sl256],
                              w_in[:, o_ga:o_ga + 16], np.zeros((D, NFM - 1040), np.float32)], axis=1)
        wtm = np.concatenate([w_in[:, o_dav:o_dav + 1024][:, sl256], w_in[:, o_gv:o_gv + 1024][:, sl256], gk_w], axis=1)
        qkg = np.stack([np.tile(inp["da_q_norm_g"][0], 2), np.tile(inp["da_k_norm_g"][0], 2)], axis=1)
        lam = np.stack([inp["da_lambda_q1"][0], inp["da_lambda_k1"][0], inp["da_lambda_q2"][0], inp["da_lambda_k2"][0]], 0)[None]
        aup = np.concatenate([inp["gla_gate_up"][0][:, sl128], inp["gla_gate_bias"][0][None, sl128], np.zeros((15, 128), np.float32)], axis=0)
        gout = inp["gla_out_norm_g"][0].reshape(2, 128).T
        maps.append({
            "x": np.ascontiguousarray(inp["x"][b, :S_LEN]),
            "pos": np.ascontiguousarray(inp["positions"][b:b + 1, :S_LEN]).astype(np.int32),
            "c": np.ascontiguousarray(inp["c"][b].reshape(NCH, 128).T),
            "wada": np.ascontiguousarray(inp["w_ada"][0][:, 0:2 * D]),
            "bada": np.ascontiguousarray(inp["b_ada"][0][None, 0:2 * D]),
            "g1": np.ascontiguousarray(inp["norm1_g"][0][None]),
            "wfm": np.ascontiguousarray(wfm, dtype=np.float32),
            "wtm": np.ascontiguousarray(wtm, dtype=np.float32),
            "consts": consts, "fcol": fcol,
            "qkg": np.ascontiguousarray(qkg, dtype=np.float32),
            "lam": np.ascontiguousarray(lam, dtype=np.float32),
            "subg": np.ascontiguousarray(inp["da_subln_g"][0][:, None]),
            "aup": np.ascontiguousarray(aup, dtype=np.float32),
            "gout": np.ascontiguousarray(gout, dtype=np.float32),
        })
    return maps


def run_l1(inp, S_LEN):
    nc = build_l1(S_LEN)
    res = run_bass_kernel_spmd(nc, l1_inputs(inp, S_LEN), core_ids=list(range(8)))
    yaT = np.zeros((2, 1024, S_LEN), ml_dtypes.bfloat16)
    ybT = np.zeros((2, 1024, S_LEN), ml_dtypes.bfloat16)
    for core in range(8):
        b, hg = core // 4, core % 4
        yaT[b, 256 * hg:256 * hg + 256] = res.results[core]["yaT"]
        ybT[b, 256 * hg:256 * hg + 256] = res.results[core]["ybT"]
    return yaT, ybT


def build_l2(N, env=None):
    NG = N // 512
    if env is None:
        nc = bass.Bass("TRN2", target_bir_lowering=False)
        x_d = nc.dram_tensor("x", [N, D], F32, kind="ExternalInput").ap()
        yaT_d = nc.dram_tensor("yaT", [1024, N], BF16, kind="ExternalInput").ap()
        ybT_d = nc.dram_tensor("ybT", [1024, N], BF16, kind="ExternalInput").ap()
        c_d = nc.dram_tensor("c", [128, NCH], F32, kind="ExternalInput").ap()
        wada_d = nc.dram_tensor("wada", [D, 6 * D], F32, kind="ExternalInput").ap()
        bada_d = nc.dram_tensor("bada", [1, 6 * D], F32, kind="ExternalInput").ap()
    else:
        nc = env["nc"]
        x_d = nc.dram_tensor("l2_x", [N, D], F32, kind="ExternalInput").ap()
        c_d, wada_d, bada_d = env["c"], env["wada"], env["bada"]
    g12_d = nc.dram_tensor("g12", [1, 2 * D], F32, kind="ExternalInput").ap()
    wg_d = nc.dram_tensor("wg", [D, 2 * D], F32, kind="ExternalInput").ap()
    wpa_d = nc.dram_tensor("wpa", [1024, D], F32, kind="ExternalInput").ap()
    wpb_d = nc.dram_tensor("wpb", [1024, D], F32, kind="ExternalInput").ap()
    wo_d = nc.dram_tensor("wo", [D, D], F32, kind="ExternalInput").ap()
    w1_d = nc.dram_tensor("w1", [D, 4 * D], F32, kind="ExternalInput").ap()
    w2_d = nc.dram_tensor("w2", [4 * D, D], F32, kind="ExternalInput").ap()
    ident_d = nc.dram_tensor("ident", [128, 128], F32, kind="ExternalInput").ap()
    out_d = nc.dram_tensor("out", [N, D], F32, kind="ExternalOutput").ap()

    if env is None:
        cx = Ctx(nc)
        ps = [nc.alloc_psum_tensor("ps%d" % i, [128, 512], F32).ap() for i in range(8)]
        psb = [Buf("ps%d" % i, excl=True) for i in range(8)]
    else:
        cx, ps, psb = env["cx"], env["ps"], env["psb"]
    S = cx.S
    with ExitStack() as gst:
        identf, b_identf = cx.alloc(gst, [128, 128], F32, "identf")
        ident, b_ident = cx.alloc(gst, [128, 128], BF16, "ident")
        ones_f, b_ones_f = cx.alloc(gst, [128, 128], F32, "ones_f")
        d_c = S.new_dma_sem("d_c")
        S.dma("sp", S.new_dma_sem("d_cx8"), identf, ident_d, writes=[b_identf])
        S.op("dve", lambda: nc.vector.tensor_copy(out=ident, in_=identf), [b_identf], [b_ident])
        S.op("pool", lambda: nc.gpsimd.memset(ones_f, 1.0), [], [b_ones_f])
        one11 = ones_f[0:1, 0:1]
        cols, b_cols = cx.alloc(gst, [128, 6, NCH], F32, "cols")
        with ExitStack() as st:
            row, b_row = compute_mod_cols(cx, st, ps, psb, c_d, wada_d, bada_d, 6 * D, None, None, one11, b_ones_f)
            g12, b_g12 = cx.alloc(st, [1, 2 * D], F32, "g12")
            S.dma("sp", S.new_dma_sem("d_cx9"), g12, g12_d, writes=[b_g12])
            for (ci, gi) in ((1, 0), (4, 1)):
                S.op("dve", lambda: nc.vector.scalar_tensor_tensor(out=row[:, ci * D:(ci + 1) * D], in0=row[:, ci * D:(ci + 1) * D], scalar=1.0,
                                                                   in1=g12[:, gi * D:(gi + 1) * D], op0=ALU.add, op1=ALU.mult), [b_row, b_g12], [b_row])
            for ci in range(6):
                row_to_cols(cx, ps[1 + ci % 2], psb[1 + ci % 2], row[:, ci * D:(ci + 1) * D], b_row, NCH, one11, b_ones_f, cols[:, ci, :], b_cols)
            S.barrier()
        sh1, sc1, gt1, sh2, sc2, gt2 = [cols[:, i, :] for i in range(6)]

        xs, b_xs_all = cx.alloc(gst, [128, 4, D], F32, "xs")
        b_xs = [Buf("xs%d" % i) for i in range(4)]
        d_xs = [S.new_dma_sem("d_xs%d" % i) for i in range(4)]
        d_out = [S.new_dma_sem("d_out%d" % i) for i in range(4)]
        junk, b_junk = cx.alloc(gst, [128, D], BF16, "junk")
        sss = [cx.alloc(gst, [128, 1], F32, "ss") for _ in range(2)]
        xns = [cx.alloc(gst, [128, D], BF16, "xn") for _ in range(2)]
        uT, b_uT = cx.alloc(gst, [128, NCH, 512], BF16, "uT")
        hid, b_hid = cx.alloc(gst, [128, 64, 512], BF16, "hid")
        mergedT = hid[:, 0:16, :]
        yT = [hid[:, 16:24, :], hid[:, 24:32, :]]
        d_y = S.new_dma_sem("d_y")
        NW = 3
        stg = [cx.alloc(gst, [128, NCH, 128], F32, "stg") for _ in range(NW)]
        wbf = [cx.alloc(gst, [128, NCH, 128], BF16, "wbf") for _ in range(NW)]
        d_w = [S.new_dma_sem("d_w%d" % i) for i in range(NW)]
        sg = [cx.alloc(gst, [128, 512], F32, "sg") for _ in range(2)]
        tmpa, b_tmpa = cx.alloc(gst, [128, 512], F32, "tmpa")
        moTs = [cx.alloc(gst, [128, 512], F32, "moT") for _ in range(2)]
        rl, b_rl = cx.alloc(gst, [128, 512], F32, "rl")
        wi = [0]

        def stream(src, kc_n):
            i = wi[0] % NW
            wi[0] += 1
            sa, sb = stg[i]
            wa, wb = wbf[i]
            S.dma("sp", d_w[i], sa[:, 0:kc_n, :], src.rearrange("(kc p) m -> p kc m", p=128), writes=[sb])
            if wi[0] % 4 != 0:
                S.op("dve", lambda: nc.vector.tensor_copy(out=wa[:, 0:kc_n, :], in_=sa[:, 0:kc_n, :]), [sb], [wb])
            else:
                S.op("pool", lambda: nc.gpsimd.tensor_copy(out=wa[:, 0:kc_n, :], in_=sa[:, 0:kc_n, :]), [sb], [wb])
            return wa, wb

        acc_i = [0]

        def fm_matmul(w_ap, w_b, rhs_fn, rhs_b, kc_n, pacc=None, first=True, last=True):
            if pacc is None:
                k = 2 + acc_i[0] % 2
                acc_i[0] += 1
                pacc = (ps[k], psb[k])
            for kc in range(kc_n):
                S.op("pe", lambda: nc.tensor.matmul(pacc[0], lhsT=w_ap[:, kc, :], rhs=rhs_fn(kc), start=(first and kc == 0), stop=(last and kc == kc_n - 1)),
                     [w_b, rhs_b], [pacc[1]], signal=(last and kc == kc_n - 1))
            return pacc

        tr_i = [0]

        def gate_transpose_add(pacc, gcol, mc):
            mo, b_mo = moTs[tr_i[0] % 2]
            pT4, pT4b = ps[4 + tr_i[0] % 2], psb[4 + tr_i[0] % 2]
            tr_i[0] += 1
            S.op("act", lambda: nc.scalar.activation(out=mo, in_=pacc[0], func=AF.Copy, scale=gcol[:, mc:mc + 1]), [pacc[1], b_cols], [b_mo])
            for tt in range(4):
                S.op("pe", lambda: nc.tensor.transpose(out=pT4[:, tt * 128:(tt + 1) * 128], in_=mo[:, tt * 128:(tt + 1) * 128], identity=identf),
                     [b_mo, b_identf], [pT4b], signal=(tt == 3))
            for tt in range(4):
                S.op("dve", lambda: nc.vector.tensor_tensor(out=xs[:, tt, mc * 128:(mc + 1) * 128], in0=pT4[:, tt * 128:(tt + 1) * 128],
                                                            in1=xs[:, tt, mc * 128:(mc + 1) * 128], op=ALU.add), [pT4b, b_xs[tt]], [b_xs[tt]])

        def norm_T(tt, xi, scol, shcol):
            ss, b_ss = sss[xi % 2]
            xn, b_xn = xns[xi % 2]
            xt = xs[:, tt, :]
            S.op("act", lambda: nc.scalar.activation(out=junk, in_=xt, func=AF.Square, accum_out=ss), [b_xs[tt]], [b_junk, b_ss])
            rsqrt_inplace(cx, ss, b_ss, ss, b_ss, 1.0 / D)
            S.op("act", lambda: nc.scalar.activation(out=xn, in_=xt, func=AF.Identity, scale=ss), [b_xs[tt], b_ss], [b_xn])
            for half in range(2):
                pT = ps[half].bitcast(BF16)
                for k in range(8):
                    c = half * 8 + k
                    S.op("pe", lambda: nc.tensor.transpose(out=pT[:, k * 128:(k + 1) * 128], in_=xn[:, c * 128:(c + 1) * 128], identity=ident),
                         [b_xn, b_ident], [psb[half]], signal=(k == 7))
                for k in range(8):
                    c = half * 8 + k
                    S.op("act", lambda: nc.scalar.activation(out=uT[:, c, tt * 128:(tt + 1) * 128], in_=pT[:, k * 128:(k + 1) * 128], func=AF.Identity,
                                                             scale=scol[:, c:c + 1], bias=shcol[:, c:c + 1]), [psb[half], b_cols], [b_uT])

        xi = 0
        for g in range(NG):
            r0 = g * 512
            for tt in range(4):
                S.dma("sp" if tt % 2 == 0 else "pool", d_xs[tt], xs[:, tt, :], x_d[r0 + tt * 128:r0 + (tt + 1) * 128, :], writes=[b_xs[tt]])
                norm_T(tt, xi, sc1, sh1)
                xi += 1
            if env is None:
                for w, src in enumerate((yaT_d, ybT_d)):
                    S.dma("sp", d_y, yT[w], src[:, r0:r0 + 512].rearrange("(kc p) n -> p kc n", p=128), writes=[b_hid])
            else:
                for w in range(2):
                    S.dma("sp", d_y, yT[w], env["mine"][w].rearrange("r h p n -> p (r h) n")[:, :, r0:r0 + 512],
                          reads=[env["b_mine"]], writes=[b_hid])
            for mc in range(NCH):
                ms = slice(mc * 128, (mc + 1) * 128)
                for w, (wp_d, goff) in enumerate(((wpa_d, 0), (wpb_d, D))):
                    wa, wb = stream(wg_d[:, goff + mc * 128:goff + (mc + 1) * 128], NCH)
                    pg = fm_matmul(wa, wb, lambda kc: uT[:, kc, :], b_uT, NCH)
                    sga, b_sga = sg[w]
                    S.op("act", lambda: nc.scalar.activation(out=sga, in_=pg[0], func=AF.Sigmoid), [pg[1]], [b_sga])
                    wa, wb = stream(wp_d[:, ms], 8)
                    pp = fm_matmul(wa, wb, lambda kc: yT[w][:, kc, :], b_hid, 8)
                    if w == 0:
                        S.op("dve", lambda: nc.vector.tensor_tensor(out=tmpa, in0=pp[0], in1=sga, op=ALU.mult), [pp[1], b_sga], [b_tmpa])
                    else:
                        S.op("dve", lambda: nc.vector.tensor_tensor(out=rl, in0=pp[0], in1=sga, op=ALU.mult), [pp[1], b_sga], [b_rl])
                        S.op("pool", lambda: nc.gpsimd.tensor_tensor(out=mergedT[:, mc, :], in0=rl, in1=tmpa, op=ALU.add), [b_rl, b_tmpa], [b_hid])
            for mc in range(NCH):
                wa, wb = stream(wo_d[:, mc * 128:(mc + 1) * 128], NCH)
                pm = fm_matmul(wa, wb, lambda kc: mergedT[:, kc, :], b_hid, NCH)
                gate_transpose_add(pm, gt1, mc)
            for tt in range(4):
                norm_T(tt, xi, sc2, sh2)
                xi += 1
            for fc in range(64):
                wa, wb = stream(w1_d[:, fc * 128:(fc + 1) * 128], NCH)
                ph = fm_matmul(wa, wb, lambda kc: uT[:, kc, :], b_uT, NCH)
                S.op("act", lambda: nc.scalar.activation(out=rl, in_=ph[0], func=AF.Relu), [ph[1]], [b_rl])
                S.op("pool", lambda: nc.gpsimd.tensor_tensor(out=hid[:, fc, :], in0=rl, in1=rl, op=ALU.mult), [b_rl], [b_hid])
            for mc in range(NCH):
                k = 2 + acc_i[0] % 2
                acc_i[0] += 1
                pacc = (ps[k], psb[k])
                for q4 in range(4):
                    wa, wb = stream(w2_d[q4 * 2048:(q4 + 1) * 2048, mc * 128:(mc + 1) * 128], NCH)
                    fm_matmul(wa, wb, lambda kc: hid[:, q4 * 16 + kc, :], b_hid, NCH, pacc=pacc, first=(q4 == 0), last=(q4 == 3))
                gate_transpose_add(pacc, gt2, mc)
            for tt in range(4):
                S.dma("sp" if tt % 2 == 0 else "pool", d_out[tt], out_d[r0 + tt * 128:r0 + (tt + 1) * 128, :], xs[:, tt, :], reads=[b_xs[tt]])
        S.finish()
    return nc


def build_fused(S_LEN):
    N = S_LEN // 4
    nc = bass.Bass("TRN2", target_bir_lowering=False)
    cx = Ctx(nc)
    S = cx.S
    env = {"nc": nc, "cx": cx}
    env["ps"] = [nc.alloc_psum_tensor("ps%d" % i, [128, 512], F32).ap() for i in range(8)]
    env["psb"] = [Buf("ps%d" % i, excl=True) for i in range(8)]
    env["x"] = nc.dram_tensor("x", [S_LEN, D], F32, kind="ExternalInput").ap()
    env["c"] = nc.dram_tensor("c", [128, NCH], F32, kind="ExternalInput").ap()
    env["wada"] = nc.dram_tensor("wada", [D, 6 * D], F32, kind="ExternalInput").ap()
    env["bada"] = nc.dram_tensor("bada", [1, 6 * D], F32, kind="ExternalInput").ap()
    NCK = S_LEN // CW
    ycat_t = nc.dram_tensor("ycat", [NCK, 512, CW], BF16)
    gath_t = nc.dram_tensor("gath", [NCK, 4 * 512, CW], BF16)
    env["ycat"] = ycat_t.ap()
    env["gath"] = gath_t.ap()
    build_l1(S_LEN, env=env)
    cc_sem = nc.alloc_semaphore("cc_sem")
    g = nc.gpsimd
    for i in range(NCK):
        g.collective_compute("AllGather", ALU.bypass, replica_groups=[[0, 1, 2, 3], [4, 5, 6, 7]],
                             ins=[ycat_t.ap()[i].opt()], outs=[gath_t.ap()[i].opt()]).then_inc(cc_sem)
        g.wait_ge(cc_sem, i + 1)
    b_gath = Buf("gath")
    b_gath.w = (cc_sem, NCK)
    S.waited["pool"][cc_sem] = NCK
    mine = nc.dram_tensor("mine", [2, 4, 2, 128, N], BF16).ap()
    me = g.partition_id() % 4
    gv = gath_t.ap().rearrange("(j cc) (r w h p) s -> r w h j p cc s", j=4, r=4, w=2, h=2, p=128)
    d_mine = S.new_dma_sem("d_mine")
    b_mine = Buf("mine")
    for w in range(2):
        for r in range(4):
            for hh in range(2):
                S.dma("pool", d_mine, mine[w, r, hh].rearrange("p (cc s) -> p cc s", s=CW), gv[r, w, hh][bass.ds(me, 1)][0],
                      reads=[b_gath], writes=[b_mine])
    env["mine"] = mine
    env["b_mine"] = b_mine
    build_l2(N, env=env)
    return nc


def l2_inputs(inp, yaT, ybT, N, S_LEN):
    ident = np.eye(128, dtype=np.float32)
    w_in = inp["w_in"][0]
    wg = np.ascontiguousarray(w_in[:, 6160:6160 + 2 * D])
    maps = []
    for core in range(8):
        b, j = core // 4, core % 4
        sl = slice(j * N, (j + 1) * N)
        maps.append({
            "x": np.ascontiguousarray(inp["x"][b, :S_LEN][sl]),
            "yaT": np.ascontiguousarray(yaT[b][:, sl]),
            "ybT": np.ascontiguousarray(ybT[b][:, sl]),
            "c": np.ascontiguousarray(inp["c"][b].reshape(NCH, 128).T),
            "wada": np.ascontiguousarray(inp["w_ada"][0]),
            "bada": np.ascontiguousarray(inp["b_ada"][0][None]),
            "g12": np.ascontiguousarray(np.concatenate([inp["norm1_g"][0], inp["norm2_g"][0]])[None]),
            "wg": wg,
            "wpa": np.ascontiguousarray(inp["w_branch_a"][0]),
            "wpb": np.ascontiguousarray(inp["w_branch_b"][0]),
            "wo": np.ascontiguousarray(inp["w_out"][0]),
            "w1": np.ascontiguousarray(inp["w_mlp_in"][0]),
            "w2": np.ascontiguousarray(inp["w_mlp_out"][0]),
            "ident": ident,
        })
    return maps


def run_l2(inp, yaT, ybT, S_LEN):
    N = S_LEN // 4
    nc = build_l2(N)
    res = run_bass_kernel_spmd(nc, l2_inputs(inp, yaT, ybT, N, S_LEN), core_ids=list(range(8)))
    out = np.zeros((2, S_LEN, D), np.float32)
    for core in range(8):
        b, j = core // 4, core % 4
        out[b, j * N:(j + 1) * N] = res.results[core]["out"]
    return out


def fused_inputs(inp, S_LEN):
    N = S_LEN // 4
    m1 = l1_inputs(inp, S_LEN)
    dummy = np.zeros((2, 1, 4), ml_dtypes.bfloat16)
    maps = []
    wada = np.ascontiguousarray(inp["w_ada"][0])
    bada = np.ascontiguousarray(inp["b_ada"][0][None])
    ident = np.eye(128, dtype=np.float32)
    wg = np.ascontiguousarray(inp["w_in"][0][:, 6160:6160 + 2 * D])
    g12 = np.ascontiguousarray(np.concatenate([inp["norm1_g"][0], inp["norm2_g"][0]])[None])
    for core in range(8):
        b, j = core // 4, core % 4
        m = dict(m1[core])
        m["wada"] = wada
        m["bada"] = bada
        m["l2_x"] = np.ascontiguousarray(inp["x"][b, :S_LEN][j * N:(j + 1) * N])
        m["g12"] = g12
        m["wg"] = wg
        m["wpa"] = np.ascontiguousarray(inp["w_branch_a"][0])
        m["wpb"] = np.ascontiguousarray(inp["w_branch_b"][0])
        m["wo"] = np.ascontiguousarray(inp["w_out"][0])
        m["w1"] = np.ascontiguousarray(inp["w_mlp_in"][0])
        m["w2"] = np.ascontiguousarray(inp["w_mlp_out"][0])
        m["ident"] = ident
        maps.append(m)
    return maps


def run_fused(inp, S_LEN):
    N = S_LEN // 4
    nc = build_fused(S_LEN)
    res = run_bass_kernel_spmd(nc, fused_inputs(inp, S_LEN), core_ids=list(range(8)))
    out = np.zeros((2, S_LEN, D), np.float32)
    for core in range(8):
        b, j = core // 4, core % 4
        out[b, j * N:(j + 1) * N] = res.results[core]["out"]
    return out


def kernel(**inputs):
    inp = {k: np.asarray(v) for k, v in inputs.items()}
    S_LEN = inp["x"].shape[1]
    return run_fused(inp, S_LEN)
```
